# Optimizing a Trainium2 kernel written in Bass

```python
import math
import jax, jax.numpy as jnp
from jax import lax
import numpy as np

D_MODEL = 1024
BATCH = 2
SEQ = 8192
DEPTH = 4

GRID_W = 64
N_HEADS = 16
HEAD_DIM = D_MODEL // N_HEADS
MAX_WIN_ROWS = 8
WIN_COLS = 16
COL_QBLOCK = 16
COL_BAND = COL_QBLOCK + WIN_COLS
NEG_INF = -1e30
CONV_WIDTH = 3
PEER_HEADS = 8
PEER_N_KEYS = 128
PEER_N_EXPERTS = PEER_N_KEYS * PEER_N_KEYS
PEER_TOPK = 16
PEER_KEY_DIM = 256
PEER_HALF_DIM = PEER_KEY_DIM // 2
PEER_TOK_CHUNK = 128
N_MIXERS = 2
RMS_EPS = 1e-6

kernel_name = "hybrid_natten_shortconv_peer_encoder"


def rms_norm(x, gain):
    x32 = x.astype(jnp.float32)
    y = x32 * lax.rsqrt(jnp.mean(x32 * x32, axis=-1, keepdims=True) + RMS_EPS)
    return (y * gain.astype(jnp.float32)).astype(x.dtype)


def _col_window_tables():
    n_blk = GRID_W // COL_QBLOCK
    qcol = np.arange(GRID_W).reshape(n_blk, COL_QBLOCK)
    band_start = np.clip(np.arange(n_blk) * COL_QBLOCK - WIN_COLS // 2, 0, GRID_W - COL_BAND)
    band_cols = band_start[:, None] + np.arange(COL_BAND)
    win_start = np.clip(qcol - WIN_COLS // 2, 0, GRID_W - WIN_COLS)
    kc = band_cols[:, None, :]
    mask = (kc >= win_start[..., None]) & (kc < win_start[..., None] + WIN_COLS)
    col_off = np.clip(kc - qcol[..., None] + WIN_COLS - 1, 0, 2 * WIN_COLS - 2)
    return (jnp.asarray(band_cols, jnp.int32), jnp.asarray(mask),
            jnp.asarray(col_off, jnp.int32))


def neighbourhood_attention(h, w_qkv, w_o, rel_bias):
    b, s, d = h.shape
    rows = s // GRID_W
    win_rows = min(MAX_WIN_ROWS, rows)
    n_cblk = GRID_W // COL_QBLOCK
    band_cols, col_mask, col_off = _col_window_tables()
    qkv = (h @ w_qkv).reshape(b, rows, GRID_W, 3, N_HEADS, HEAD_DIM)
    qkv = qkv.transpose(3, 0, 4, 1, 2, 5)
    q = qkv[0] * (1.0 / math.sqrt(HEAD_DIM))
    k = qkv[1]
    v = qkv[2]

    def one_row(r):
        rs = jnp.clip(r - win_rows // 2, 0, rows - win_rows)
        q_r = lax.dynamic_index_in_dim(q, r, axis=2, keepdims=False)
        q_r = q_r.reshape(b, N_HEADS, n_cblk, COL_QBLOCK, HEAD_DIM)
        k_band = lax.dynamic_slice_in_dim(k, rs, win_rows, axis=2)[:, :, :, band_cols]
        v_band = lax.dynamic_slice_in_dim(v, rs, win_rows, axis=2)[:, :, :, band_cols]
        scores = jnp.einsum('bhjqd,bhrjkd->bhjqrk', q_r.astype(jnp.float32),
                            k_band.astype(jnp.float32))
        row_off = rs + jnp.arange(win_rows) - r + MAX_WIN_ROWS - 1
        bias = rel_bias[:, row_off[None, None, :, None], col_off[:, :, None, :]]
        scores = jnp.where(col_mask[:, :, None, :], scores + bias.astype(jnp.float32), NEG_INF)
        p = jax.nn.softmax(scores, axis=(-2, -1)).astype(v.dtype)
        o = jnp.einsum('bhjqrk,bhrjkd->bhjqd', p, v_band)
        return o.reshape(b, N_HEADS, GRID_W, HEAD_DIM)

    out = lax.map(one_row, jnp.arange(rows))
    out = out.transpose(1, 0, 3, 2, 4).reshape(b, s, d)
    return out @ w_o


def short_conv_mixer(h, w_in, w_conv, w_out):
    s = h.shape[1]
    gate_b, gate_c, val = jnp.split(h @ w_in, 3, axis=-1)
    u = gate_c * val
    pad = CONV_WIDTH // 2
    up = jnp.pad(u, ((0, 0), (pad, CONV_WIDTH - 1 - pad), (0, 0)))
    conv = sum(up[:, j:j + s] * w_conv[j] for j in range(CONV_WIDTH))
    return (gate_b * conv) @ w_out


def peer_mixer(h, w_query, sub_keys, w_up, w_down):
    b, s, d = h.shape
    xt = h.reshape(-1, PEER_TOK_CHUNK, d)
    keys32 = sub_keys.astype(jnp.float32)

    def chunk(xc):
        c = xc.shape[0]
        q = (xc @ w_query).reshape(c, PEER_HEADS, 2, PEER_HALF_DIM).astype(jnp.float32)
        sc = jnp.einsum('chpd,hpnd->chpn', q, keys32)
        sv, si = lax.top_k(sc, PEER_TOPK)
        cand = (sv[:, :, 0, :, None] + sv[:, :, 1, None, :]).reshape(c, PEER_HEADS, -1)
        cidx = (si[:, :, 0, :, None] * PEER_N_KEYS + si[:, :, 1, None, :]).reshape(c, PEER_HEADS, -1)
        best, pos = lax.top_k(cand, PEER_TOPK)
        eidx = jnp.take_along_axis(cidx, pos, axis=-1)
        g = jax.nn.softmax(best, axis=-1)
        u = w_up[eidx]
        a = jnp.einsum('cd,chkd->chk', xc, u).astype(jnp.float32)
        act = (jax.nn.gelu(a, approximate=False) * g).astype(xc.dtype)
        vv = w_down[eidx]
        return jnp.einsum('chk,chkd->cd', act, vv)

    return lax.map(chunk, xt).reshape(b, s, d)


def setup_inputs(seed: int = 0) -> dict:
    key = jax.random.key(seed)
    ks = jax.random.split(key, 16)
    n_attn = (DEPTH + 1) // 2
    n_conv = DEPTH // 2
    d = D_MODEL
    nrm = jax.random.normal
    f32 = jnp.float32
    return {
        "x": nrm(ks[0], (BATCH, SEQ, d), f32),
        "norm_mix": 1.0 + 0.02 * nrm(ks[1], (DEPTH, d), f32),
        "norm_ffn": 1.0 + 0.02 * nrm(ks[2], (DEPTH, d), f32),
        "norm_final": 1.0 + 0.02 * nrm(ks[3], (d,), f32),
        "attn_w_qkv": nrm(ks[4], (n_attn, d, 3 * d), f32) * d ** -0.5,
        "attn_w_o": nrm(ks[5], (n_attn, d, d), f32) * d ** -0.5,
        "attn_rel_bias": 0.1 * nrm(ks[6], (n_attn, N_HEADS, 2 * MAX_WIN_ROWS - 1, 2 * WIN_COLS - 1), f32),
        "conv_w_in": nrm(ks[7], (n_conv, d, 3 * d), f32) * d ** -0.5,
        "conv_w_conv": nrm(ks[8], (n_conv, CONV_WIDTH, d), f32) * CONV_WIDTH ** -0.5,
        "conv_w_out": nrm(ks[9], (n_conv, d, d), f32) * d ** -0.5,
        "peer_w_query": nrm(ks[10], (DEPTH, d, PEER_HEADS * PEER_KEY_DIM), f32) * d ** -0.5,
        "peer_sub_keys": nrm(ks[11], (DEPTH, PEER_HEADS, 2, PEER_N_KEYS, PEER_HALF_DIM), f32) * PEER_HALF_DIM ** -0.5,
        "peer_w_up": nrm(ks[12], (DEPTH, PEER_N_EXPERTS, d), f32) * d ** -0.5,
        "peer_w_down": nrm(ks[13], (DEPTH, PEER_N_EXPERTS, d), f32) * (PEER_HEADS * PEER_TOPK) ** -0.5,
    }


def reference(x, norm_mix, norm_ffn, norm_final, attn_w_qkv, attn_w_o, attn_rel_bias,
              conv_w_in, conv_w_conv, conv_w_out, peer_w_query, peer_sub_keys,
              peer_w_up, peer_w_down):
    h = x
    for i in range(DEPTH):
        j = i // N_MIXERS
        hn = rms_norm(h, norm_mix[i])
        if i % N_MIXERS == 0:
            h = h + neighbourhood_attention(hn, attn_w_qkv[j], attn_w_o[j], attn_rel_bias[j])
        else:
            h = h + short_conv_mixer(hn, conv_w_in[j], conv_w_conv[j], conv_w_out[j])
        hn = rms_norm(h, norm_ffn[i])
        h = h + peer_mixer(hn, peer_w_query[i], peer_sub_keys[i], peer_w_up[i], peer_w_down[i])
    return rms_norm(h, norm_final)
```

```python
import contextlib
import numpy as np
import concourse.bass as bass
import concourse.mybir as mybir
from concourse.bass_utils import run_bass_kernel_spmd

F32 = mybir.dt.float32
BF16 = mybir.dt.bfloat16
U32 = mybir.dt.uint32
ALU = mybir.AluOpType
AF = mybir.ActivationFunctionType
AX = mybir.AxisListType

HALO = 8
EXT_ROWS = 32 + 2 * HALO
EXT_TOK = EXT_ROWS * 64
OWN0 = HALO * 64
NT = 16
NEG = -1e30
XT = 32
XOFF = 8
TROWS = 48
CHUNKS = [[(-4, 16), (12, 8)], [(-3, 16), (13, 6)], [(-2, 16), (14, 4)], [(0, 16)]]


class Buf:
    __slots__ = ("lw", "rd")

    def __init__(self):
        self.lw = None
        self.rd = []


class FW:
    ENGS = ("pe", "dve", "act", "pool", "sp")

    def __init__(self, nc):
        self.nc = nc
        self.ins = {e: [] for e in self.ENGS}
        self.dma_cnt = {}
        self.dma_keys = []

    def _deps(self, reads, writes):
        deps = []
        for b in reads:
            if b.lw is not None:
                deps.append(b.lw)
        for b in writes:
            if b.lw is not None:
                deps.append(b.lw)
            deps.extend(b.rd)
        return deps

    def op(self, eng, fn, reads=(), writes=()):
        deps = self._deps(reads, writes)
        idx = len(self.ins[eng])
        self.ins[eng].append([fn, deps, None])
        tok = ("e", eng, idx)
        for b in writes:
            b.lw = tok
            b.rd = []
        for b in reads:
            b.rd.append(tok)
        return tok

    def dma(self, eng, key, fn, reads=(), writes=()):
        deps = self._deps(reads, writes)
        if key not in self.dma_cnt:
            self.dma_cnt[key] = 0
            self.dma_keys.append(key)
        self.dma_cnt[key] += 16
        tok = ("d", key, self.dma_cnt[key])
        self.ins[eng].append([fn, deps, key])
        for b in writes:
            b.lw = tok
            b.rd = []
        for b in reads:
            b.rd.append(tok)
        return tok

    def barrier(self):
        toks = []
        for e in self.ENGS:
            for idx in range(len(self.ins[e]) - 1, -1, -1):
                fn, deps, key = self.ins[e][idx]
                if fn is not None and key is None:
                    toks.append(("e", e, idx))
                    break
        for k in self.dma_keys:
            toks.append(("d", k, self.dma_cnt[k]))
        for e in self.ENGS:
            self.ins[e].append([None, list(toks), None])

    def finalize(self):
        nc = self.nc
        needed = {e: set() for e in self.ENGS}
        for e in self.ENGS:
            for idx, (fn, deps, key) in enumerate(self.ins[e]):
                for d in deps:
                    if d[0] == "e" and (d[1] != e or key is not None or e != "pe"):
                        needed[d[1]].add(d[2])
        rank = {}
        for e in self.ENGS:
            r = 0
            for idx in range(len(self.ins[e])):
                if idx in needed[e]:
                    r += 1
                    rank[(e, idx)] = r
        es = contextlib.ExitStack()
        esem = {e: es.enter_context(nc.semaphore("s_" + e)) for e in self.ENGS}
        dsem = {k: es.enter_context(nc.semaphore("d_%d" % i)) for i, k in enumerate(self.dma_keys)}
        block = es.enter_context(nc.Block())
        ins = self.ins

        def make(e):
            def body(engine):
                waited = {}
                for idx, (fn, deps, key) in enumerate(ins[e]):
                    for d in deps:
                        if d[0] == "e":
                            if d[1] == e and key is None and e == "pe":
                                continue
                            wk = ("e", d[1])
                            val = rank[(d[1], d[2])]
                            sem = esem[d[1]]
                        else:
                            wk = ("d", d[1])
                            val = d[2]
                            sem = dsem[d[1]]
                        if waited.get(wk, 0) >= val:
                            continue
                        waited[wk] = val
                        engine.wait_ge(sem, val)
                    if fn is None:
                        continue
                    inst = fn(engine)
                    if key is not None:
                        inst.then_inc(dsem[key], 16)
                    elif idx in needed[e]:
                        inst.then_inc(esem[e], 1)
            return body

        block.tensor(make("pe"))
        block.vector(make("dve"))
        block.scalar(make("act"))
        block.gpsimd(make("pool"))
        block.sync(make("sp"))
        es.close()


def build_fused():
    nc = bass.Bass("TRN2", target_bir_lowering=False)
    fw = FW(nc)

    def din(name, shape, dt=F32):
        return nc.dram_tensor(name, shape, dt, kind="ExternalInput").ap()

    hx = din("hx", [XT * 128, 1024])
    valid = din("valid", [128, XT])
    g_fin = din("g_fin", [128, 1024])
    L = []
    for i in range(4):
        d = {"g_mix": din("g_mix%d" % i, [128, 1024]), "g_ffn": din("g_ffn%d" % i, [128, 1024]),
             "wq": din("wq%d" % i, [1024, 2048]), "keysT": din("keysT%d" % i, [128, 2048]),
             "wupT": din("wupT%d" % i, [1024, 16384]), "wdown": din("wdown%d" % i, [16384, 1024])}
        if i % 2 == 1:
            d["w_in"] = din("w_in%d" % i, [1024, 3072])
            d["w_cv"] = din("w_cv%d" % i, [128, 24])
            d["w_out"] = din("w_out%d" % i, [1024, 1024])
        else:
            d["w_qkv"] = din("w_qkv%d" % i, [1024, 3072])
            d["w_o"] = din("w_o%d" % i, [1024, 1024])
            d["btab"] = din("btab%d" % i, [TROWS, 8, 64, 2048])
        L.append(d)
    hout = nc.dram_tensor("hout", [2048, 1024], F32, kind="ExternalOutput").ap()
    gscr = nc.dram_tensor("gscr", [NT, 2, 128, 8192], BF16).ap()
    HD = [nc.dram_tensor("hd%d" % i, [XT * 128, 1024], F32).ap() for i in range(2)]

    top = contextlib.ExitStack()

    _cnt = [0]

    def sb(es, name, shape, dt):
        _cnt[0] += 1
        return es.enter_context(nc.sbuf_tensor("%s_%d" % (name, _cnt[0]), shape, dt))

    P = [top.enter_context(nc.psum_tensor("ps%d" % i, [128, 512], F32)) for i in range(8)]
    PB = [Buf() for _ in range(8)]

    h_sb = sb(top, "h_sb", [128, NT, 1024], F32)
    HB = [Buf() for _ in range(NT)]
    ident = sb(top, "ident", [128, 128], F32)
    identb = sb(top, "identb", [128, 128], BF16)
    iot = sb(top, "iot", [128, 128], F32)
    iotb = sb(top, "iotb", [128, 128], BF16)
    pidx = sb(top, "pidx", [128, 1], F32)
    CONST = Buf()

    def pe_mm(out, lhsT, rhs, start, stop, reads, writes):
        fw.op("pe", lambda e: e.matmul(out, lhsT, rhs, start=start, stop=stop), reads, writes)

    def pe_tr(out, in_, idn, reads, writes):
        fw.op("pe", lambda e: e.transpose(out=out, in_=in_, identity=idn), reads, writes)

    def tt(eng, out, in0, in1, op, reads, writes):
        fw.op(eng, lambda e: e.tensor_tensor(out=out, in0=in0, in1=in1, op=op), reads, writes)

    def ts(eng, out, in0, s1, s2, op0, op1, reads, writes):
        if op1 is None:
            fw.op(eng, lambda e: e.tensor_scalar(out=out, in0=in0, scalar1=s1, scalar2=None, op0=op0), reads, writes)
        else:
            fw.op(eng, lambda e: e.tensor_scalar(out=out, in0=in0, scalar1=s1, scalar2=s2, op0=op0, op1=op1), reads, writes)

    def stt(eng, out, in0, scalar, in1, op0, op1, reads, writes):
        fw.op(eng, lambda e: e.scalar_tensor_tensor(out=out, in0=in0, scalar=scalar, in1=in1, op0=op0, op1=op1), reads, writes)

    def cp(eng, out, in_, reads, writes):
        if eng == "act":
            fw.op(eng, lambda e: e.copy(out=out, in_=in_), reads, writes)
        else:
            fw.op(eng, lambda e: e.tensor_copy(out=out, in_=in_), reads, writes)

    def act(out, in_, func, reads, writes, bias=None, scale=None, accum_out=None):
        kw = {}
        if bias is not None:
            kw["bias"] = bias
        if scale is not None:
            kw["scale"] = scale
        if accum_out is not None:
            kw["accum_out"] = accum_out
        fw.op("act", lambda e: e.activation(out=out, in_=in_, func=func, **kw), reads, writes)

    def dma(eng, key, out, in_, reads, writes):
        fw.dma(eng, key, lambda e: e.dma_start(out=out, in_=in_), reads, writes)

    fw.op("pool", lambda e: e.iota(iot[:], pattern=[[1, 128]], base=0, channel_multiplier=0,
                                   allow_small_or_imprecise_dtypes=True), writes=[CONST])
    fw.op("pool", lambda e: e.iota(pidx[:], pattern=[[0, 1]], base=0, channel_multiplier=1,
                                   allow_small_or_imprecise_dtypes=True), writes=[CONST])
    ts("dve", ident[:], iot[:], pidx[:, 0:1], None, ALU.is_equal, None, [CONST], [CONST])
    cp("dve", identb[:], ident[:], [CONST], [CONST])
    cp("dve", iotb[:], iot[:], [CONST], [CONST])

    def rmsnorm(es_bufs, src_ap, src_bufs, gain_t, gain_buf, out_ap, out_buf):
        junk, JB, st, SB_ = es_bufs
        act(junk[:], src_ap, AF.Square, src_bufs, [JB, SB_], accum_out=st[:, 0:1])
        ts("dve", st[:, 1:2], st[:, 0:1], 1.0 / 1024.0, 1e-6, ALU.mult, ALU.add, [SB_], [SB_])
        act(st[:, 2:3], st[:, 1:2], AF.Sqrt, [SB_], [SB_])
        fw.op("dve", lambda e: e.reciprocal(out=st[:, 3:4], in_=st[:, 2:3]), [SB_], [SB_])
        stt("dve", out_ap, src_ap, st[:, 3:4], gain_t[:], ALU.mult, ALU.mult, src_bufs + [SB_, gain_buf], [out_buf])

    def to_featmajor(src, src_buf, dstT, col0, dst_buf, banks=(0, 1)):
        for half in range(2):
            b = banks[half]
            for kk in range(4):
                k = half * 4 + kk
                pe_tr(P[b][:, kk * 128:(kk + 1) * 128], src[:, k * 128:(k + 1) * 128], ident[:],
                      [src_buf, CONST], [PB[b]])
            cp("act", dstT[:, half * 4:(half + 1) * 4, col0:col0 + 128],
               P[b][:].rearrange("p (k t) -> p k t", k=4), [PB[b]], [dst_buf])

    def load_w_bf(w_ap, col0, ncols, dst, dst_col0, dst_buf, stage, stage_buf, key, piece=256):
        wv = w_ap.rearrange("(k p) n -> p k n", p=128)
        for c in range(0, ncols, piece):
            dma("sp", key, stage[:, :, 0:piece], wv[:, :, col0 + c: col0 + c + piece], [], [stage_buf])
            cp("pool", dst[:, :, dst_col0 + c: dst_col0 + c + piece], stage[:, :, 0:piece], [stage_buf], [dst_buf])

    def emit_chunk(li, kind, src, dst, o0, n, final):
        W = L[li]
        g_mix, g_ffn = W['g_mix'], W['g_ffn']
        wq, keysT, wupT, wdown = W['wq'], W['keysT'], W['wupT'], W['wdown']
        ET = (n + 8) * 128
        HB = [Buf() for _ in range(NT)]
        for t in range(n):
            dma('sp', 'h%d' % (t % 4), h_sb[:, t, :], src[(o0 + XOFF + t) * 128:(o0 + XOFF + t + 1) * 128, :], [], [HB[t]])
        if kind == "conv":
            mx = contextlib.ExitStack()
            hnT = sb(mx, "c_hnT", [128, 8, 2304], BF16)
            HNT = Buf()
            gain = sb(mx, "c_gain", [128, 1024], F32)
            GB = Buf()
            junk = sb(mx, "c_junk", [128, 1024], BF16)
            st = sb(mx, "c_st", [128, 4], F32)
            hn = sb(mx, "c_hn", [128, 1024], F32)
            xt = sb(mx, "c_xt", [128, 1024], F32)
            JB, SB_, HN, XTB = Buf(), Buf(), Buf(), Buf()
            dma("sp", "gain", gain[:], g_mix, [], [GB])
            w_in, w_cv, w_out = W["w_in"], W["w_cv"], W["w_out"]
            NU = (n + 2) * 128
            for i in range(n + 2):
                if 1 <= i < n + 1:
                    t = i - 1
                    tsrc, sbufs = h_sb[:, t, :], [HB[t]]
                else:
                    at = o0 + XOFF - 1 + i
                    dma("sp", "cx", xt[:], src[at * 128:(at + 1) * 128, :], [], [XTB])
                    tsrc, sbufs = xt[:], [XTB]
                rmsnorm((junk, JB, st, SB_), tsrc, sbufs, gain, GB, hn[:], HN)
                to_featmajor(hn, HN, hnT, i * 128, HNT)
            stage = sb(mx, "c_stage", [128, 8, 128], F32)
            STG = Buf()
            wcv = sb(mx, "c_wcv", [128, 24], F32)
            WCV = Buf()
            dma("sp", "wcv", wcv[:], w_cv, [], [WCV])
            zT = sb(mx, "c_zT", [128, 8, 2048], BF16)
            ZT = Buf()
            wtri = sb(mx, "c_wtri", [128, 8, 384], BF16)
            WTRI = Buf()
            u = sb(mx, "c_u", [128, 2304], F32)
            gbt = sb(mx, "c_gb", [128, 2304], F32)
            gct = sb(mx, "c_gc", [128, 384], F32)
            t1 = sb(mx, "c_t1", [128, 2048], F32)
            U, GBT, GCT, T1 = Buf(), Buf(), Buf(), Buf()
            for c in range(8):
                for j in range(3):
                    load_w_bf(w_in, j * 1024 + c * 128, 128, wtri, j * 128, WTRI, stage, STG, "wst", piece=128)
                for tc_ in range((n + 2) // 2):
                    c0 = tc_ * 256
                    for j in range(3):
                        b = (tc_ % 2) * 3 + j
                        for k in range(8):
                            pe_mm(P[b][:, 0:256], wtri[:, k, j * 128:(j + 1) * 128], hnT[:, k, c0:c0 + 256],
                                  k == 0, k == 7, [WTRI, HNT], [PB[b]])
                    b0 = (tc_ % 2) * 3
                    cp("act", gbt[:, c0:c0 + 256], P[b0][:, 0:256], [PB[b0]], [GBT])
                    cp("act", gct[:, 0:256], P[b0 + 1][:, 0:256], [PB[b0 + 1]], [GCT])
                    tt("dve", u[:, c0:c0 + 256], gct[:, 0:256], P[b0 + 2][:, 0:256], ALU.mult, [GCT, PB[b0 + 2]], [U])
                NO = n * 128
                ts("dve", t1[:, 0:NO], u[:, 128:128 + NO], wcv[:, 8 + c:9 + c], None, ALU.mult, None, [U, WCV], [T1])
                stt("dve", t1[:, 0:NO], u[:, 127:127 + NO], wcv[:, c:c + 1], t1[:, 0:NO], ALU.mult, ALU.add, [U, WCV, T1], [T1])
                stt("dve", t1[:, 0:NO], u[:, 129:129 + NO], wcv[:, 16 + c:17 + c], t1[:, 0:NO], ALU.mult, ALU.add, [U, WCV, T1], [T1])
                tt("dve", zT[:, c, 0:NO], t1[:, 0:NO], gbt[:, 128:128 + NO], ALU.mult, [T1, GBT], [ZT])
            wo_bf = sb(mx, "c_wo", [128, 8, 1024], BF16)
            WO = Buf()
            load_w_bf(w_out, 0, 1024, wo_bf, 0, WO, stage, STG, "wst", piece=128)
            for t in range(n):
                for half in range(2):
                    b = 6 + half
                    for k in range(8):
                        pe_mm(P[b][:], zT[:, k, t * 128:(t + 1) * 128], wo_bf[:, k, half * 512:(half + 1) * 512],
                              k == 0, k == 7, [ZT, WO], [PB[b]])
                    tt("dve", h_sb[:, t, half * 512:(half + 1) * 512], h_sb[:, t, half * 512:(half + 1) * 512],
                       P[b][:], ALU.add, [HB[t], PB[b]], [HB[t]])
            fw.barrier()
            mx.close()

        if kind == "attn":
            mx = contextlib.ExitStack()
            m1 = contextlib.ExitStack()
            aoT = sb(mx, "a_aoT", [128, 8, 2048], BF16)
            AOT = Buf()
            hnT = sb(m1, "a_hnT", [128, 8, EXT_TOK], BF16)
            HNT = Buf()
            gain = sb(m1, "a_gain", [128, 1024], F32)
            GB = Buf()
            junk = sb(m1, "a_junk", [128, 1024], BF16)
            st = sb(m1, "a_st", [128, 4], F32)
            hn = sb(m1, "a_hn", [128, 1024], F32)
            xt = sb(m1, "a_xt", [128, 1024], F32)
            JB, SB_, HN, XTB = Buf(), Buf(), Buf(), Buf()
            dma("sp", "gain", gain[:], g_mix, [], [GB])
            w_qkv, w_o, btab = W["w_qkv"], W["w_o"], W["btab"]
            for i in range(n + 8):
                if 4 <= i < n + 4:
                    t = i - 4
                    tsrc, sbufs = h_sb[:, t, :], [HB[t]]
                else:
                    at = o0 + XOFF - 4 + i
                    dma("sp", "cx", xt[:], src[at * 128:(at + 1) * 128, :], [], [XTB])
                    tsrc, sbufs = xt[:], [XTB]
                rmsnorm((junk, JB, st, SB_), tsrc, sbufs, gain, GB, hn[:], HN)
                to_featmajor(hn, HN, hnT, i * 128, HNT)
            stage = sb(m1, "a_stage", [128, 8, 128], F32)
            STG = Buf()
            wtri = sb(m1, "a_wtri", [128, 8, 384], BF16)
            WTRI = Buf()
            QT = sb(m1, "a_QT", [128, 2048], BF16)
            KT = sb(m1, "a_KT", [128, EXT_TOK], BF16)
            V = sb(m1, "a_V", [128, EXT_TOK // 128, 128], BF16)
            QTB, KTB, VB = Buf(), Buf(), Buf()
            tab = sb(m1, "a_tab", [64, 2048], F32)
            TAB = [Buf()]
            s_sb = sb(m1, "a_s", [64, 1024], F32)
            p_bf = sb(m1, "a_p", [64, 1024], BF16)
            pT = sb(m1, "a_pT", [128, 8, 64], BF16)
            sm = sb(m1, "a_sm", [64, 8], F32)
            o_sb = sb(m1, "a_o", [64, 128], F32)
            S, PBF, PT, SM, OSB = Buf(), Buf(), Buf(), Buf(), Buf()
            for hp in range(8):
                for j in range(3):
                    load_w_bf(w_qkv, j * 1024 + hp * 128, 128, wtri, j * 128, WTRI, stage, STG, "wst", piece=128)
                for c4 in range(n // 4):
                    b = c4 % 2
                    for k in range(8):
                        pe_mm(P[b][:], wtri[:, k, 0:128], hnT[:, k, OWN0 + c4 * 512: OWN0 + (c4 + 1) * 512],
                              k == 0, k == 7, [WTRI, HNT], [PB[b]])
                    cp("act", QT[:, c4 * 512:(c4 + 1) * 512], P[b][:], [PB[b]], [QTB])
                for c6 in range((n + 8) // 4):
                    b = c6 % 2
                    for k in range(8):
                        pe_mm(P[b][:], wtri[:, k, 128:256], hnT[:, k, c6 * 512:(c6 + 1) * 512],
                              k == 0, k == 7, [WTRI, HNT], [PB[b]])
                    cp("act", KT[:, c6 * 512:(c6 + 1) * 512], P[b][:], [PB[b]], [KTB])
                for vt in range(n + 8):
                    b = 2 + (vt // 4) % 2
                    q4 = vt % 4
                    for k in range(8):
                        pe_mm(P[b][:, q4 * 128:(q4 + 1) * 128], hnT[:, k, vt * 128:(vt + 1) * 128], wtri[:, k, 256:384],
                              k == 0, k == 7, [WTRI, HNT], [PB[b]])
                    if q4 == 3:
                        cp("act", V[:, vt - 3:vt + 1, :], P[b][:].rearrange("p (a n) -> p a n", a=4), [PB[b]], [VB])
                for lr in range(2 * n):
                    e_ = HALO + lr
                    kb = e_ - 8 if e_ % 2 == 0 else e_ - 9
                    dma("sp", "tab", tab[:], btab[2 * o0 + lr + 8, hp], [], TAB)
                    for hh in range(2):
                        pl = hh * 64
                        for half in range(2):
                            b = 4 + half
                            pe_mm(P[b][0:64, :], QT[pl:pl + 64, lr * 64:(lr + 1) * 64],
                                  KT[pl:pl + 64, kb * 64 + half * 512: kb * 64 + (half + 1) * 512],
                                  True, True, [QTB, KTB], [PB[b]])
                            stt("dve", s_sb[:, half * 512:(half + 1) * 512], P[b][0:64, :], 0.125,
                                tab[:, hh * 1024 + half * 512: hh * 1024 + (half + 1) * 512], ALU.mult, ALU.add,
                                [PB[b]] + TAB, [S])
                        fw.op("dve", lambda e: e.reduce_max(out=sm[:, 0:1], in_=s_sb[:], axis=AX.X), [S], [SM])
                        ts("dve", sm[:, 1:2], sm[:, 0:1], -1.0, None, ALU.mult, None, [SM], [SM])
                        act(p_bf[:], s_sb[:], AF.Exp, [S, SM], [PBF, SM], bias=sm[:, 1:2], accum_out=sm[:, 2:3])
                        pTp = P[6][:].bitcast(BF16)
                        for c in range(8):
                            pe_tr(pTp[:, c * 64:(c + 1) * 64], p_bf[:, c * 128:(c + 1) * 128], identb[0:64, 0:64],
                                  [PBF, CONST], [PB[6]])
                        cp("dve", pT[:], pTp[:, 0:512].rearrange("p (c q) -> p c q", c=8), [PB[6]], [PT])
                        for c in range(8):
                            pe_mm(P[7][0:64, hh * 64:(hh + 1) * 64], pT[:, c, :], V[:, kb // 2 + c, hh * 64:(hh + 1) * 64],
                                  c == 0, c == 7, [PT, VB], [PB[7]])
                        fw.op("dve", lambda e: e.reciprocal(out=sm[:, 3:4], in_=sm[:, 2:3]), [SM], [SM])
                        ts("dve", o_sb[:, hh * 64:(hh + 1) * 64], P[7][0:64, hh * 64:(hh + 1) * 64], sm[:, 3:4], None,
                           ALU.mult, None, [PB[7], SM], [OSB])
                    pe_tr(P[3][:, 0:64], o_sb[:], ident[0:64, 0:64], [OSB, CONST], [PB[3]])
                    cp("act", aoT[:, hp, lr * 64:(lr + 1) * 64], P[3][:, 0:64], [PB[3]], [AOT])
            fw.barrier()
            m1.close()
            stage2 = sb(mx, "a_stage2", [128, 8, 256], F32)
            STG2 = Buf()
            wo_bf = sb(mx, "a_wo", [128, 8, 1024], BF16)
            WO = Buf()
            load_w_bf(w_o, 0, 1024, wo_bf, 0, WO, stage2, STG2, "wst2")
            for t in range(n):
                for half in range(2):
                    b = half
                    for k in range(8):
                        pe_mm(P[b][:], aoT[:, k, t * 128:(t + 1) * 128], wo_bf[:, k, half * 512:(half + 1) * 512],
                              k == 0, k == 7, [AOT, WO], [PB[b]])
                    tt("dve", h_sb[:, t, half * 512:(half + 1) * 512], h_sb[:, t, half * 512:(half + 1) * 512],
                       P[b][:], ALU.add, [HB[t], PB[b]], [HB[t]])
            fw.barrier()
            mx.close()

        pz = contextlib.ExitStack()
        hnT = sb(pz, "p_hnT", [128, 8, 2048], BF16)
        HNT = [Buf() for _ in range(NT)]
        p0 = contextlib.ExitStack()
        gain = sb(p0, "p_gain", [128, 1024], F32)
        GB = Buf()
        junk = sb(p0, "p_junk", [128, 1024], BF16)
        st = sb(p0, "p_st", [128, 4], F32)
        hn = sb(p0, "p_hn", [128, 1024], F32)
        JB, SB_, HN = Buf(), Buf(), Buf()
        dma("sp", "gain", gain[:], g_ffn, [], [GB])
        for t in range(n):
            rmsnorm((junk, JB, st, SB_), h_sb[:, t, :], [HB[t]], gain, GB, hn[:], HN)
            to_featmajor(hn, HN, hnT, t * 128, HNT[t])
        fw.barrier()
        p0.close()

        p1 = contextlib.ExitStack()
        stage = sb(p1, "p_stage", [128, 8, 256], F32)
        STG = Buf()
        wq_bf = sb(p1, "p_wq", [128, 8, 2048], BF16)
        WQ = Buf()
        load_w_bf(wq, 0, 2048, wq_bf, 0, WQ, stage, STG, "wst")
        kT_bf = sb(p1, "p_kT", [128, 16, 128], BF16)
        KTB = Buf()
        dma("sp", "wst", stage[:].rearrange("p k n -> p (k n)"), keysT, [], [STG])
        cp("pool", kT_bf[:].rearrange("p c n -> p (c n)"), stage[:].rearrange("p k n -> p (k n)"), [STG], [KTB])
        qT_sb = sb(p1, "p_qT", [128, 16, 128], BF16)
        sc_sb = sb(p1, "p_sc", [128, 16, 128], F32)
        sc2 = sb(p1, "p_sc2", [128, 128], F32)
        sv = sb(p1, "p_sv", [128, 16, 16], F32)
        si = sb(p1, "p_si", [128, 16, 16], U32)
        cand = sb(p1, "p_cand", [128, 8, 256], F32)
        cand2 = sb(p1, "p_cand2", [128, 256], F32)
        best = sb(p1, "p_best", [128, 8, 16], F32)
        pos = sb(p1, "p_pos", [128, 8, 16], U32)
        gex = sb(p1, "p_gex", [128, 8, 16], F32)
        gz = sb(p1, "p_gz", [128, 16], F32)
        gate = sb(p1, "p_gate", [128, 8, 16], F32)
        au = sb(p1, "p_au", [128, 8, 16], U32)
        bu = sb(p1, "p_bu", [128, 8, 16], U32)
        af = sb(p1, "p_af", [128, 8, 16], F32)
        bf = sb(p1, "p_bf", [128, 8, 16], F32)
        sif = sb(p1, "p_sif", [128, 16, 16], F32)
        i_f = sb(p1, "p_if", [128, 128], F32)
        j_f = sb(p1, "p_jf", [128, 128], F32)
        iT = sb(p1, "p_iT", [128, 128], BF16)
        jT = sb(p1, "p_jT", [128, 128], BF16)
        gT = sb(p1, "p_gT", [128, 128], BF16)
        A_s = sb(p1, "p_A", [128, 16, 128], BF16)
        B_s = sb(p1, "p_B", [128, 16, 128], BF16)
        G_sb = sb(p1, "p_G", [128, 128, 64], BF16)
        QTB = [Buf() for _ in range(4)]
        SCB = [Buf() for _ in range(4)]
        R = Buf()
        TR = Buf()
        AB, BB, GSB = Buf(), Buf(), Buf()
        GSCR = [[Buf(), Buf()] for _ in range(NT)]
        svv = sv[:].rearrange("p (h two) k -> p h two k", two=2)
        siv = sif[:].rearrange("p (h two) k -> p h two k", two=2)
        iot16 = iot[:, 0:16].unsqueeze(1).unsqueeze(1).to_broadcast([128, 8, 16, 16])
        iot_b = iotb[:, :].unsqueeze(1).to_broadcast([128, 16, 128])
        cand4 = cand[:].rearrange("p h (a b) -> p h a b", a=16)
        for t in range(n):
            for cb in range(4):
                for cc in range(4):
                    c = cb * 4 + cc
                    for k in range(8):
                        pe_mm(P[cb][:, cc * 128:(cc + 1) * 128], wq_bf[:, k, c * 128:(c + 1) * 128],
                              hnT[:, k, t * 128:(t + 1) * 128], k == 0, k == 7, [WQ, HNT[t]], [PB[cb]])
                cp("act", qT_sb[:, cb * 4:(cb + 1) * 4, :], P[cb][:].rearrange("p (c n) -> p c n", c=4), [PB[cb]], [QTB[cb]])
            for cb in range(4):
                for cc in range(4):
                    c = cb * 4 + cc
                    pe_mm(P[4 + cb][:, cc * 128:(cc + 1) * 128], qT_sb[:, c, :], kT_bf[:, c, :], True, True,
                          [QTB[cb], KTB], [PB[4 + cb]])
                cp("act", sc_sb[:, cb * 4:(cb + 1) * 4, :], P[4 + cb][:].rearrange("p (c n) -> p c n", c=4),
                   [PB[4 + cb]], [SCB[cb]])

            def top16(vals_ap, scratch_ap, out_v, out_i, rbufs):
                fw.op("dve", lambda e: e.max(out=out_v[:, 0:8], in_=vals_ap), rbufs + [R], [R])
                fw.op("dve", lambda e: e.max_index(out=out_i[:, 0:8], in_max=out_v[:, 0:8], in_values=vals_ap), rbufs + [R], [R])
                fw.op("dve", lambda e: e.match_replace(out=scratch_ap, in_to_replace=out_v[:, 0:8], in_values=vals_ap,
                                                       imm_value=NEG), rbufs + [R], [R])
                fw.op("dve", lambda e: e.max(out=out_v[:, 8:16], in_=scratch_ap), [R], [R])
                fw.op("dve", lambda e: e.max_index(out=out_i[:, 8:16], in_max=out_v[:, 8:16], in_values=scratch_ap), [R], [R])

            for c in range(16):
                top16(sc_sb[:, c, :], sc2[:], sv[:, c, :], si[:, c, :], [SCB[c // 4]])
            tt("dve", cand4, svv[:, :, 0, :].unsqueeze(3).to_broadcast([128, 8, 16, 16]),
               svv[:, :, 1, :].unsqueeze(2).to_broadcast([128, 8, 16, 16]), ALU.add, [R], [R])
            for h in range(8):
                top16(cand[:, h, :], cand2[:], best[:, h, :], pos[:, h, :], [])
            tt("dve", gex[:], best[:], best[:, :, 0:1].to_broadcast([128, 8, 16]), ALU.subtract, [R], [R])
            act(gex[:], gex[:], AF.Exp, [R], [R])
            fw.op("dve", lambda e: e.reduce_sum(out=gz[:, 0:8], in_=gex[:], axis=AX.X), [R], [R])
            fw.op("dve", lambda e: e.reciprocal(out=gz[:, 8:16], in_=gz[:, 0:8]), [R], [R])
            tt("dve", gate[:], gex[:], gz[:, 8:16].unsqueeze(2).to_broadcast([128, 8, 16]), ALU.mult, [R], [R])
            fw.op("dve", lambda e: e.tensor_single_scalar(out=au[:], in_=pos[:], scalar=4, op=ALU.logical_shift_right), [R], [R])
            fw.op("dve", lambda e: e.tensor_single_scalar(out=bu[:], in_=pos[:], scalar=15, op=ALU.bitwise_and), [R], [R])
            cp("dve", af[:], au[:], [R], [R])
            cp("dve", bf[:], bu[:], [R], [R])
            cp("dve", sif[:], si[:], [R], [R])
            for (xf, half, dsti) in ((af, 0, i_f), (bf, 1, j_f)):
                tt("dve", cand4, xf[:].unsqueeze(3).to_broadcast([128, 8, 16, 16]), iot16, ALU.is_equal, [R, CONST], [R])
                tt("dve", cand4, cand4, siv[:, :, half, :].unsqueeze(2).to_broadcast([128, 8, 16, 16]), ALU.mult, [R], [R])
                fw.op("dve", (lambda d_: lambda e: e.tensor_reduce(
                    out=d_[:], in_=cand[:].rearrange("p h (k a) -> p (h k) a", a=16), axis=AX.X, op=ALU.add))(dsti), [R], [R])
            for (srci, dstT, b) in ((i_f[:], iT, 0), (j_f[:], jT, 1), (gate[:].rearrange("p h k -> p (h k)"), gT, 2)):
                pe_tr(P[b][:, 0:128], srci, ident[:], [R, CONST], [PB[b]])
                cp("act", dstT[:], P[b][:, 0:128], [PB[b]], [TR])
            for hb in range(2):
                for sbk in range(4):
                    t0 = hb * 64 + sbk * 16
                    tt("dve", A_s[:], iot_b, iT[:, t0:t0 + 16].unsqueeze(2).to_broadcast([128, 16, 128]), ALU.is_equal,
                       [TR, CONST], [AB])
                    tt("dve", A_s[:], A_s[:], gT[:, t0:t0 + 16].unsqueeze(2).to_broadcast([128, 16, 128]), ALU.mult,
                       [TR, AB], [AB])
                    tt("dve", B_s[:], iot_b, jT[:, t0:t0 + 16].unsqueeze(2).to_broadcast([128, 16, 128]), ALU.is_equal,
                       [TR, CONST], [BB])
                    for q in range(4):
                        b = q
                        for x in range(4):
                            tl = q * 4 + x
                            pe_mm(P[b][:, x * 128:(x + 1) * 128], B_s[:, tl, :], A_s[:, tl, :], True, True, [AB, BB], [PB[b]])
                        tk = sbk * 16 + q * 4
                        cp("act", G_sb[:, :, tk:tk + 4].rearrange("p n t -> p t n"),
                           P[b][:].rearrange("p (t n) -> p t n", t=4), [PB[b]], [GSB])
                dma("pool", "gst", gscr[t, hb], G_sb[:].rearrange("p n t -> p (n t)"), [GSB], [GSCR[t][hb]])
        fw.barrier()
        p1.close()

        p2 = contextlib.ExitStack()
        stU = sb(p2, "e_stU", [128, 8, 512], F32)
        stD = sb(p2, "e_stD", [128, 4, 1024], F32)
        wu = [sb(p2, "e_wu%d" % i, [128, 8, 512], BF16) for i in range(2)]
        wd = [sb(p2, "e_wd%d" % i, [128, 4, 1024], BF16) for i in range(2)]
        gsl = [sb(p2, "e_gs%d" % i, [128, 2, 4, 64], BF16) for i in range(3)]
        ge = [sb(p2, "e_ge%d" % i, [128, 4, 128], F32) for i in range(3)]
        ab = [sb(p2, "e_ab%d" % i, [128, 4, 128], BF16) for i in range(3)]
        STU, STD = Buf(), Buf()
        WU, WD = [Buf(), Buf()], [Buf(), Buf()]
        GSL, GE, ABB = [Buf() for _ in range(3)], [Buf() for _ in range(3)], [Buf() for _ in range(3)]
        wupv = wupT.rearrange("(k p) e -> p k e", p=128)
        wdnv = wdown.rearrange("(n q) d -> q n d", q=128)
        NG = 32
        its = [(g, t) for g in range(NG) for t in range(n)]

        def load_group(g):
            wpar = g % 2
            dma("sp", "wu", stU[:], wupv[:, :, g * 512:(g + 1) * 512], [], [STU])
            cp("pool", wu[wpar][:], stU[:], [STU], [WU[wpar]])
            dma("sp", "wd", stD[:], wdnv[:, g * 4:(g + 1) * 4, :], [], [STD])
            cp("pool", wd[wpar][:], stD[:], [STD], [WD[wpar]])

        load_group(0)

        def stage_up(k):
            g, t = its[k]
            r3, wpar = k % 3, g % 2
            if t == 2 and g + 1 < NG:
                load_group(g + 1)
            for hb in range(2):
                dma("sp", "gsl%d" % r3, gsl[r3][:, hb].rearrange("p n t -> p (n t)"),
                    gscr[t, hb][:, g * 256:(g + 1) * 256], [GSCR[t][hb]], [GSL[r3]])
            for nn in range(4):
                for k8 in range(8):
                    pe_mm(P[r3][:, nn * 128:(nn + 1) * 128], wu[wpar][:, k8, nn * 128:(nn + 1) * 128],
                          hnT[:, k8, t * 128:(t + 1) * 128], k8 == 0, k8 == 7, [WU[wpar], HNT[t]], [PB[r3]])

        def stage_mid(k):
            r3 = k % 3
            act(ge[r3][:], P[r3][:].rearrange("p (n t) -> p n t", n=4), AF.Gelu, [PB[r3]], [GE[r3]])
            tt("dve", ab[r3][:].rearrange("p n (h t) -> p n h t", h=2),
               ge[r3][:].rearrange("p n (h t) -> p n h t", h=2),
               gsl[r3][:].rearrange("p h n t -> p n h t"), ALU.mult, [GE[r3], GSL[r3]], [ABB[r3]])

        def stage_down(k):
            g, t = its[k]
            r3, par, wpar = k % 3, k % 2, g % 2
            for nn in range(4):
                for half in range(2):
                    b = 3 + par * 2 + half
                    pe_mm(P[b][:], ab[r3][:, nn, :], wd[wpar][:, nn, half * 512:(half + 1) * 512], nn == 0, nn == 3,
                          [ABB[r3], WD[wpar]], [PB[b]])
            for half in range(2):
                b = 3 + par * 2 + half
                tt("dve", h_sb[:, t, half * 512:(half + 1) * 512], h_sb[:, t, half * 512:(half + 1) * 512], P[b][:],
                   ALU.add, [HB[t], PB[b]], [HB[t]])

        NI = len(its)
        for k in range(NI + 2):
            if k < NI:
                stage_up(k)
            if 1 <= k < NI + 1:
                stage_mid(k - 1)
            if k >= 2:
                stage_down(k - 2)
        fw.barrier()
        p2.close()

        pf = contextlib.ExitStack()
        ob = [sb(pf, "f_o%d" % i, [128, 1024], F32) for i in range(2)]
        OB = [Buf(), Buf()]
        if final:
            gain = sb(pf, "f_gain", [128, 1024], F32)
            GB = Buf()
            junk = sb(pf, "f_junk", [128, 1024], BF16)
            st = sb(pf, "f_st", [128, 4], F32)
            JB, SB_ = Buf(), Buf()
            dma("sp", "gain", gain[:], g_fin, [], [GB])
            for t in range(n):
                rmsnorm((junk, JB, st, SB_), h_sb[:, t, :], [HB[t]], gain, GB, ob[t % 2][:], OB[t % 2])
                dma("sp", "out%d" % (t % 2), hout[(o0 + t) * 128:(o0 + t + 1) * 128, :], ob[t % 2][:], [OB[t % 2]], [])
        else:
            for t in range(n):
                at = o0 + XOFF + t
                ts("dve", ob[t % 2][:], h_sb[:, t, :], vmask[:, at:at + 1], None, ALU.mult, None, [HB[t], VM], [OB[t % 2]])
                dma("sp", "out%d" % (t % 2), dst[at * 128:(at + 1) * 128, :], ob[t % 2][:], [OB[t % 2]], [])
        fw.barrier()
        pf.close()
        pz.close()

    vmask = sb(top, 'vmask', [128, XT], F32)
    VM = Buf()
    dma('sp', 'vm', vmask[:], valid, [], [VM])
    zs = contextlib.ExitStack()
    zt = sb(zs, 'zt', [128, 1024], F32)
    ZB = Buf()
    fw.op('dve', lambda e: e.memset(zt[:], 0.0), [], [ZB])
    for d_ in range(2):
        for t in range(XT):
            dma('sp', 'z%d' % (t % 2), HD[d_][t * 128:(t + 1) * 128, :], zt[:], [ZB], [])
    fw.barrier()
    zs.close()
    src = hx
    for li in range(4):
        kind = 'attn' if li % 2 == 0 else 'conv'
        dst = HD[li % 2]
        for (o0, n) in CHUNKS[li]:
            emit_chunk(li, kind, src, dst, o0, n, li == 3)
        src = dst
    fw.finalize()
    top.close()
    return nc


_PROG = []


def _ext(h, c):
    b, q = c // 4, c % 4
    hb = h[b].reshape(128, 64, 1024)
    out = np.zeros((XT * 2, 64, 1024), np.float32)
    r0 = 32 * q - 2 * XOFF
    lo, hi = max(r0, 0), min(r0 + XT * 2, 128)
    out[lo - r0:hi - r0] = hb[lo:hi]
    return out.reshape(XT * 128, 1024)


def _btab(rel_bias, q):
    qc = np.arange(64)[:, None]
    kc = np.arange(64)[None, :]
    ws = np.clip(qc - 8, 0, 48)
    vcol = (kc >= ws) & (kc < ws + 16)
    coff = np.clip(kc - qc + 15, 0, 30)
    out = np.full((TROWS, 16, 64, 16, 64), NEG, np.float32)
    for idx in range(TROWS):
        rr = idx - 8
        r = 32 * q + rr
        if r < 0 or r >= 128:
            continue
        kb = rr - 8 if rr % 2 == 0 else rr - 9
        kabs = 32 * q + kb
        rs = min(max(r - 4, 0), 120)
        for wr in range(16):
            krow = kabs + wr
            if rs <= krow < rs + 8:
                roff = krow - r + 7
                vals = rel_bias[:, roff, :][:, coff]
                out[idx, :, :, wr, :] = np.where(vcol[None], vals, np.float32(NEG))
    out = out.reshape(TROWS, 8, 2, 64, 1024).transpose(0, 1, 3, 2, 4).reshape(TROWS, 8, 64, 2048)
    return np.ascontiguousarray(out)


def _rep(v):
    return np.ascontiguousarray(np.broadcast_to(np.asarray(v, np.float32)[None, :], (128, 1024)))


def kernel(**inp):
    if not _PROG:
        _PROG.append(build_fused())
    nc = _PROG[0]
    h = np.ascontiguousarray(np.asarray(inp["x"], np.float32))
    common = {"g_fin": _rep(inp["norm_final"])}
    tabs = {}
    for i in range(4):
        j = i // 2
        common["g_mix%d" % i] = _rep(inp["norm_mix"][i])
        common["g_ffn%d" % i] = _rep(inp["norm_ffn"][i])
        common["wq%d" % i] = np.ascontiguousarray(inp["peer_w_query"][i], np.float32)
        common["keysT%d" % i] = np.ascontiguousarray(
            np.asarray(inp["peer_sub_keys"][i], np.float32).reshape(16, 128, 128).transpose(2, 0, 1).reshape(128, 2048))
        common["wupT%d" % i] = np.ascontiguousarray(np.asarray(inp["peer_w_up"][i], np.float32).T)
        common["wdown%d" % i] = np.ascontiguousarray(inp["peer_w_down"][i], np.float32)
        if i % 2 == 1:
            common["w_in%d" % i] = np.ascontiguousarray(inp["conv_w_in"][j], np.float32)
            common["w_cv%d" % i] = np.ascontiguousarray(
                np.asarray(inp["conv_w_conv"][j], np.float32).reshape(3, 8, 128).transpose(2, 0, 1).reshape(128, 24))
            common["w_out%d" % i] = np.ascontiguousarray(inp["conv_w_out"][j], np.float32)
        else:
            common["w_qkv%d" % i] = np.ascontiguousarray(inp["attn_w_qkv"][j], np.float32)
            common["w_o%d" % i] = np.ascontiguousarray(inp["attn_w_o"][j], np.float32)
            rb = np.asarray(inp["attn_rel_bias"][j], np.float32)
            tabs[i] = [_btab(rb, q) for q in range(4)]
    in_maps = []
    for c in range(8):
        q = c % 4
        m = dict(common)
        m["hx"] = _ext(h, c)
        v = np.zeros((XT,), np.float32)
        for t in range(XT):
            at = 16 * q + t - XOFF
            v[t] = 1.0 if 0 <= at < 64 else 0.0
        m["valid"] = np.ascontiguousarray(np.broadcast_to(v[None, :], (128, XT)))
        for i in (0, 2):
            m["btab%d" % i] = tabs[i][q]
        in_maps.append(m)
    res = run_bass_kernel_spmd(nc, in_maps, core_ids=list(range(8)))
    out = np.empty((2, 8192, 1024), np.float32)
    for c in range(8):
        b, q = c // 4, c % 4
        out[b, q * 2048:(q + 1) * 2048] = np.asarray(res.results[c]["hout"], np.float32)
    return out
```

```python
import contextlib
import numpy as np
import concourse.bass as bass
import concourse.mybir as mybir
from concourse.bass_utils import run_bass_kernel_spmd

F32 = mybir.dt.float32
BF16 = mybir.dt.bfloat16
U32 = mybir.dt.uint32
ALU = mybir.AluOpType
AF = mybir.ActivationFunctionType
AX = mybir.AxisListType

HALO = 8
EXT_ROWS = 32 + 2 * HALO
EXT_TOK = EXT_ROWS * 64
OWN0 = HALO * 64
NT = 16
NEG = -1e30
XT = 32
XOFF = 8
TROWS = 48
CHUNKS = [[(-4, 16), (12, 8)], [(-3, 16), (13, 6)], [(-2, 16), (14, 4)], [(0, 16)]]


class Buf:
    __slots__ = ("lw", "rd")

    def __init__(self):
        self.lw = None
        self.rd = []


class FW:
    ENGS = ("pe", "dve", "act", "pool", "sp")

    def __init__(self, nc):
        self.nc = nc
        self.ins = {e: [] for e in self.ENGS}
        self.dma_cnt = {}
        self.dma_keys = []

    def _deps(self, reads, writes):
        deps = []
        for b in reads:
            if b.lw is not None:
                deps.append(b.lw)
        for b in writes:
            if b.lw is not None:
                deps.append(b.lw)
            deps.extend(b.rd)
        return deps

    def op(self, eng, fn, reads=(), writes=()):
        deps = self._deps(reads, writes)
        idx = len(self.ins[eng])
        self.ins[eng].append([fn, deps, None])
        tok = ("e", eng, idx)
        for b in writes:
            b.lw = tok
            b.rd = []
        for b in reads:
            b.rd.append(tok)
        return tok

    def dma(self, eng, key, fn, reads=(), writes=()):
        deps = self._deps(reads, writes)
        if key not in self.dma_cnt:
            self.dma_cnt[key] = 0
            self.dma_keys.append(key)
        self.dma_cnt[key] += 16
        tok = ("d", key, self.dma_cnt[key])
        self.ins[eng].append([fn, deps, key])
        for b in writes:
            b.lw = tok
            b.rd = []
        for b in reads:
            b.rd.append(tok)
        return tok

    def barrier(self):
        toks = []
        for e in self.ENGS:
            for idx in range(len(self.ins[e]) - 1, -1, -1):
                fn, deps, key = self.ins[e][idx]
                if fn is not None and key is None:
                    toks.append(("e", e, idx))
                    break
        for k in self.dma_keys:
            toks.append(("d", k, self.dma_cnt[k]))
        for e in self.ENGS:
            self.ins[e].append([None, list(toks), None])

    def finalize(self):
        nc = self.nc
        needed = {e: set() for e in self.ENGS}
        for e in self.ENGS:
            for idx, (fn, deps, key) in enumerate(self.ins[e]):
                for d in deps:
                    if d[0] == "e" and (d[1] != e or key is not None or e != "pe"):
                        needed[d[1]].add(d[2])
        rank = {}
        for e in self.ENGS:
            r = 0
            for idx in range(len(self.ins[e])):
                if idx in needed[e]:
                    r += 1
                    rank[(e, idx)] = r
        es = contextlib.ExitStack()
        esem = {e: es.enter_context(nc.semaphore("s_" + e)) for e in self.ENGS}
        dsem = {k: es.enter_context(nc.semaphore("d_%d" % i)) for i, k in enumerate(self.dma_keys)}
        block = es.enter_context(nc.Block())
        ins = self.ins

        def make(e):
            def body(engine):
                waited = {}
                for idx, (fn, deps, key) in enumerate(ins[e]):
                    for d in deps:
                        if d[0] == "e":
                            if d[1] == e and key is None and e == "pe":
                                continue
                            wk = ("e", d[1])
                            val = rank[(d[1], d[2])]
                            sem = esem[d[1]]
                        else:
                            wk = ("d", d[1])
                            val = d[2]
                            sem = dsem[d[1]]
                        if waited.get(wk, 0) >= val:
                            continue
                        waited[wk] = val
                        engine.wait_ge(sem, val)
                    if fn is None:
                        continue
                    inst = fn(engine)
                    if key is not None:
                        inst.then_inc(dsem[key], 16)
                    elif idx in needed[e]:
                        inst.then_inc(esem[e], 1)
            return body

        block.tensor(make("pe"))
        block.vector(make("dve"))
        block.scalar(make("act"))
        block.gpsimd(make("pool"))
        block.sync(make("sp"))
        es.close()


def build_fused():
    nc = bass.Bass("TRN2", target_bir_lowering=False)
    fw = FW(nc)

    def din(name, shape, dt=F32):
        return nc.dram_tensor(name, shape, dt, kind="ExternalInput").ap()

    hx = din("hx", [XT * 128, 1024])
    valid = din("valid", [128, XT])
    g_fin = din("g_fin", [128, 1024])
    L = []
    for i in range(4):
        d = {"g_mix": din("g_mix%d" % i, [128, 1024]), "g_ffn": din("g_ffn%d" % i, [128, 1024]),
             "wq": din("wq%d" % i, [1024, 2048]), "keysT": din("keysT%d" % i, [128, 2048]),
             "wupT": din("wupT%d" % i, [1024, 16384]), "wdown": din("wdown%d" % i, [16384, 1024])}
        if i % 2 == 1:
            d["w_in"] = din("w_in%d" % i, [1024, 3072])
            d["w_cv"] = din("w_cv%d" % i, [128, 24])
            d["w_out"] = din("w_out%d" % i, [1024, 1024])
        else:
            d["w_qkv"] = din("w_qkv%d" % i, [1024, 3072])
            d["w_o"] = din("w_o%d" % i, [1024, 1024])
            d["btab"] = din("btab%d" % i, [TROWS, 8, 64, 2048])
        L.append(d)
    hout = nc.dram_tensor("hout", [2048, 1024], F32, kind="ExternalOutput").ap()
    gscr = nc.dram_tensor("gscr", [NT, 2, 128, 8192], BF16).ap()
    HD = [nc.dram_tensor("hd%d" % i, [XT * 128, 1024], F32).ap() for i in range(2)]

    top = contextlib.ExitStack()

    _cnt = [0]

    def sb(es, name, shape, dt):
        _cnt[0] += 1
        return es.enter_context(nc.sbuf_tensor("%s_%d" % (name, _cnt[0]), shape, dt))

    P = [top.enter_context(nc.psum_tensor("ps%d" % i, [128, 512], F32)) for i in range(8)]
    PB = [Buf() for _ in range(8)]

    h_sb = sb(top, "h_sb", [128, NT, 1024], F32)
    HB = [Buf() for _ in range(NT)]
    ident = sb(top, "ident", [128, 128], F32)
    identb = sb(top, "identb", [128, 128], BF16)
    iot = sb(top, "iot", [128, 128], F32)
    iotb = sb(top, "iotb", [128, 128], BF16)
    pidx = sb(top, "pidx", [128, 1], F32)
    CONST = Buf()

    def pe_mm(out, lhsT, rhs, start, stop, reads, writes):
        fw.op("pe", lambda e: e.matmul(out, lhsT, rhs, start=start, stop=stop), reads, writes)

    def pe_tr(out, in_, idn, reads, writes):
        fw.op("pe", lambda e: e.transpose(out=out, in_=in_, identity=idn), reads, writes)

    def tt(eng, out, in0, in1, op, reads, writes):
        fw.op(eng, lambda e: e.tensor_tensor(out=out, in0=in0, in1=in1, op=op), reads, writes)

    def ts(eng, out, in0, s1, s2, op0, op1, reads, writes):
        if op1 is None:
            fw.op(eng, lambda e: e.tensor_scalar(out=out, in0=in0, scalar1=s1, scalar2=None, op0=op0), reads, writes)
        else:
            fw.op(eng, lambda e: e.tensor_scalar(out=out, in0=in0, scalar1=s1, scalar2=s2, op0=op0, op1=op1), reads, writes)

    def stt(eng, out, in0, scalar, in1, op0, op1, reads, writes):
        fw.op(eng, lambda e: e.scalar_tensor_tensor(out=out, in0=in0, scalar=scalar, in1=in1, op0=op0, op1=op1), reads, writes)

    def cp(eng, out, in_, reads, writes):
        if eng == "act":
            fw.op(eng, lambda e: e.copy(out=out, in_=in_), reads, writes)
        else:
            fw.op(eng, lambda e: e.tensor_copy(out=out, in_=in_), reads, writes)

    def act(out, in_, func, reads, writes, bias=None, scale=None, accum_out=None):
        kw = {}
        if bias is not None:
            kw["bias"] = bias
        if scale is not None:
            kw["scale"] = scale
        if accum_out is not None:
            kw["accum_out"] = accum_out
        fw.op("act", lambda e: e.activation(out=out, in_=in_, func=func, **kw), reads, writes)

    def dma(eng, key, out, in_, reads, writes):
        fw.dma(eng, key, lambda e: e.dma_start(out=out, in_=in_), reads, writes)

    fw.op("pool", lambda e: e.iota(iot[:], pattern=[[1, 128]], base=0, channel_multiplier=0,
                                   allow_small_or_imprecise_dtypes=True), writes=[CONST])
    fw.op("pool", lambda e: e.iota(pidx[:], pattern=[[0, 1]], base=0, channel_multiplier=1,
                                   allow_small_or_imprecise_dtypes=True), writes=[CONST])
    ts("dve", ident[:], iot[:], pidx[:, 0:1], None, ALU.is_equal, None, [CONST], [CONST])
    cp("dve", identb[:], ident[:], [CONST], [CONST])
    cp("dve", iotb[:], iot[:], [CONST], [CONST])

    def rmsnorm(es_bufs, src_ap, src_bufs, gain_t, gain_buf, out_ap, out_buf):
        junk, JB, st, SB_ = es_bufs
        act(junk[:], src_ap, AF.Square, src_bufs, [JB, SB_], accum_out=st[:, 0:1])
        ts("dve", st[:, 1:2], st[:, 0:1], 1.0 / 1024.0, 1e-6, ALU.mult, ALU.add, [SB_], [SB_])
        act(st[:, 2:3], st[:, 1:2], AF.Sqrt, [SB_], [SB_])
        fw.op("dve", lambda e: e.reciprocal(out=st[:, 3:4], in_=st[:, 2:3]), [SB_], [SB_])
        stt("dve", out_ap, src_ap, st[:, 3:4], gain_t[:], ALU.mult, ALU.mult, src_bufs + [SB_, gain_buf], [out_buf])

    def to_featmajor(src, src_buf, dstT, col0, dst_buf, banks=(0, 1)):
        for half in range(2):
            b = banks[half]
            for kk in range(4):
                k = half * 4 + kk
                pe_tr(P[b][:, kk * 128:(kk + 1) * 128], src[:, k * 128:(k + 1) * 128], ident[:],
                      [src_buf, CONST], [PB[b]])
            cp("act", dstT[:, half * 4:(half + 1) * 4, col0:col0 + 128],
               P[b][:].rearrange("p (k t) -> p k t", k=4), [PB[b]], [dst_buf])

    def load_w_bf(w_ap, col0, ncols, dst, dst_col0, dst_buf, stage, stage_buf, key, piece=256):
        wv = w_ap.rearrange("(k p) n -> p k n", p=128)
        for c in range(0, ncols, piece):
            dma("sp", key, stage[:, :, 0:piece], wv[:, :, col0 + c: col0 + c + piece], [], [stage_buf])
            cp("pool", dst[:, :, dst_col0 + c: dst_col0 + c + piece], stage[:, :, 0:piece], [stage_buf], [dst_buf])

    def emit_chunk(li, kind, src, dst, o0, n, final):
        W = L[li]
        g_mix, g_ffn = W['g_mix'], W['g_ffn']
        wq, keysT, wupT, wdown = W['wq'], W['keysT'], W['wupT'], W['wdown']
        ET = (n + 8) * 128
        HB = [Buf() for _ in range(NT)]
        for t in range(n):
            dma('sp', 'h%d' % (t % 4), h_sb[:, t, :], src[(o0 + XOFF + t) * 128:(o0 + XOFF + t + 1) * 128, :], [], [HB[t]])
        if kind == "conv":
            mx = contextlib.ExitStack()
            hnT = sb(mx, "c_hnT", [128, 8, 2304], BF16)
            HNT = Buf()
            gain = sb(mx, "c_gain", [128, 1024], F32)
            GB = Buf()
            junk = sb(mx, "c_junk", [128, 1024], BF16)
            st = sb(mx, "c_st", [128, 4], F32)
            hn = sb(mx, "c_hn", [128, 1024], F32)
            xt = sb(mx, "c_xt", [128, 1024], F32)
            JB, SB_, HN, XTB = Buf(), Buf(), Buf(), Buf()
            dma("sp", "gain", gain[:], g_mix, [], [GB])
            w_in, w_cv, w_out = W["w_in"], W["w_cv"], W["w_out"]
            NU = (n + 2) * 128
            for i in range(n + 2):
                if 1 <= i < n + 1:
                    t = i - 1
                    tsrc, sbufs = h_sb[:, t, :], [HB[t]]
                else:
                    at = o0 + XOFF - 1 + i
                    dma("sp", "cx", xt[:], src[at * 128:(at + 1) * 128, :], [], [XTB])
                    tsrc, sbufs = xt[:], [XTB]
                rmsnorm((junk, JB, st, SB_), tsrc, sbufs, gain, GB, hn[:], HN)
                to_featmajor(hn, HN, hnT, i * 128, HNT)
            stage = sb(mx, "c_stage", [128, 8, 128], F32)
            STG = Buf()
            wcv = sb(mx, "c_wcv", [128, 24], F32)
            WCV = Buf()
            dma("sp", "wcv", wcv[:], w_cv, [], [WCV])
            zT = sb(mx, "c_zT", [128, 8, 2048], BF16)
            ZT = Buf()
            wtri = sb(mx, "c_wtri", [128, 8, 384], BF16)
            WTRI = Buf()
            u = sb(mx, "c_u", [128, 2304], F32)
            gbt = sb(mx, "c_gb", [128, 2304], F32)
            gct = sb(mx, "c_gc", [128, 384], F32)
            t1 = sb(mx, "c_t1", [128, 2048], F32)
            U, GBT, GCT, T1 = Buf(), Buf(), Buf(), Buf()
            for c in range(8):
                for j in range(3):
                    load_w_bf(w_in, j * 1024 + c * 128, 128, wtri, j * 128, WTRI, stage, STG, "wst", piece=128)
                for tc_ in range((n + 2) // 2):
                    c0 = tc_ * 256
                    for j in range(3):
                        b = (tc_ % 2) * 3 + j
                        for k in range(8):
                            pe_mm(P[b][:, 0:256], wtri[:, k, j * 128:(j + 1) * 128], hnT[:, k, c0:c0 + 256],
                                  k == 0, k == 7, [WTRI, HNT], [PB[b]])
                    b0 = (tc_ % 2) * 3
                    cp("act", gbt[:, c0:c0 + 256], P[b0][:, 0:256], [PB[b0]], [GBT])
                    cp("act", gct[:, 0:256], P[b0 + 1][:, 0:256], [PB[b0 + 1]], [GCT])
                    tt("dve", u[:, c0:c0 + 256], gct[:, 0:256], P[b0 + 2][:, 0:256], ALU.mult, [GCT, PB[b0 + 2]], [U])
                NO = n * 128
                ts("dve", t1[:, 0:NO], u[:, 128:128 + NO], wcv[:, 8 + c:9 + c], None, ALU.mult, None, [U, WCV], [T1])
                stt("dve", t1[:, 0:NO], u[:, 127:127 + NO], wcv[:, c:c + 1], t1[:, 0:NO], ALU.mult, ALU.add, [U, WCV, T1], [T1])
                stt("dve", t1[:, 0:NO], u[:, 129:129 + NO], wcv[:, 16 + c:17 + c], t1[:, 0:NO], ALU.mult, ALU.add, [U, WCV, T1], [T1])
                tt("dve", zT[:, c, 0:NO], t1[:, 0:NO], gbt[:, 128:128 + NO], ALU.mult, [T1, GBT], [ZT])
            wo_bf = sb(mx, "c_wo", [128, 8, 1024], BF16)
            WO = Buf()
            load_w_bf(w_out, 0, 1024, wo_bf, 0, WO, stage, STG, "wst", piece=128)
            for t in range(n):
                for half in range(2):
                    b = 6 + half
                    for k in range(8):
                        pe_mm(P[b][:], zT[:, k, t * 128:(t + 1) * 128], wo_bf[:, k, half * 512:(half + 1) * 512],
                              k == 0, k == 7, [ZT, WO], [PB[b]])
                    tt("dve", h_sb[:, t, half * 512:(half + 1) * 512], h_sb[:, t, half * 512:(half + 1) * 512],
                       P[b][:], ALU.add, [HB[t], PB[b]], [HB[t]])
            fw.barrier()
            mx.close()

        if kind == "attn":
            mx = contextlib.ExitStack()
            m1 = contextlib.ExitStack()
            aoT = sb(mx, "a_aoT", [128, 8, 2048], BF16)
            AOT = Buf()
            hnT = sb(m1, "a_hnT", [128, 8, EXT_TOK], BF16)
            HNT = Buf()
            m0 = contextlib.ExitStack()
            gain = sb(m0, "a_gain", [128, 1024], F32)
            GB = Buf()
            junk = sb(m0, "a_junk", [128, 1024], BF16)
            st = sb(m0, "a_st", [128, 4], F32)
            hn = sb(m0, "a_hn", [128, 1024], F32)
            xt = sb(m0, "a_xt", [128, 1024], F32)
            JB, SB_, HN, XTB = Buf(), Buf(), Buf(), Buf()
            dma("sp", "gain", gain[:], g_mix, [], [GB])
            w_qkv, w_o, btab = W["w_qkv"], W["w_o"], W["btab"]
            for i in range(n + 8):
                if 4 <= i < n + 4:
                    t = i - 4
                    tsrc, sbufs = h_sb[:, t, :], [HB[t]]
                else:
                    at = o0 + XOFF - 4 + i
                    dma("sp", "cx", xt[:], src[at * 128:(at + 1) * 128, :], [], [XTB])
                    tsrc, sbufs = xt[:], [XTB]
                rmsnorm((junk, JB, st, SB_), tsrc, sbufs, gain, GB, hn[:], HN)
                to_featmajor(hn, HN, hnT, i * 128, HNT)
            fw.barrier()
            m0.close()
            stage = sb(m1, "a_stage", [128, 8, 128], F32)
            STG = Buf()
            wtri = sb(m1, "a_wtri", [128, 8, 384], BF16)
            WTRI = Buf()
            QT = sb(m1, "a_QT", [128, 2048], BF16)
            KT = sb(m1, "a_KT", [128, EXT_TOK], BF16)
            V = sb(m1, "a_V", [128, EXT_TOK // 128, 128], BF16)
            QTB, KTB, VB = Buf(), Buf(), Buf()
            tab2 = [sb(m1, "a_tab%d" % i, [64, 2048], F32) for i in range(2)]
            TAB2 = [[Buf()], [Buf()]]
            s_sb2 = [sb(m1, "a_s%d" % i, [64, 1024], F32) for i in range(2)]
            p_bf2 = [sb(m1, "a_p%d" % i, [64, 1024], BF16) for i in range(2)]
            pT2 = [sb(m1, "a_pT%d" % i, [128, 8, 64], BF16) for i in range(2)]
            sm2 = [sb(m1, "a_sm%d" % i, [64, 8], F32) for i in range(2)]
            o_sb = sb(m1, "a_o", [64, 128], F32)
            S2, PBF2, PT2, SM2 = [Buf(), Buf()], [Buf(), Buf()], [Buf(), Buf()], [Buf(), Buf()]
            OSB = Buf()
            for hp in range(8):
                for j in range(3):
                    load_w_bf(w_qkv, j * 1024 + hp * 128, 128, wtri, j * 128, WTRI, stage, STG, "wst", piece=128)
                for c4 in range(n // 4):
                    b = c4 % 2
                    for k in range(8):
                        pe_mm(P[b][:], wtri[:, k, 0:128], hnT[:, k, OWN0 + c4 * 512: OWN0 + (c4 + 1) * 512],
                              k == 0, k == 7, [WTRI, HNT], [PB[b]])
                    cp("act", QT[:, c4 * 512:(c4 + 1) * 512], P[b][:], [PB[b]], [QTB])
                for c6 in range((n + 8) // 4):
                    b = c6 % 2
                    for k in range(8):
                        pe_mm(P[b][:], wtri[:, k, 128:256], hnT[:, k, c6 * 512:(c6 + 1) * 512],
                              k == 0, k == 7, [WTRI, HNT], [PB[b]])
                    cp("act", KT[:, c6 * 512:(c6 + 1) * 512], P[b][:], [PB[b]], [KTB])
                for vt in range(n + 8):
                    b = 2 + (vt // 4) % 2
                    q4 = vt % 4
                    for k in range(8):
                        pe_mm(P[b][:, q4 * 128:(q4 + 1) * 128], hnT[:, k, vt * 128:(vt + 1) * 128], wtri[:, k, 256:384],
                              k == 0, k == 7, [WTRI, HNT], [PB[b]])
                    if q4 == 3:
                        cp("act", V[:, vt - 3:vt + 1, :], P[b][:].rearrange("p (a n) -> p a n", a=4), [PB[b]], [VB])
                for lr in range(2 * n):
                    e_ = HALO + lr
                    kb = e_ - 8 if e_ % 2 == 0 else e_ - 9
                    tab, TAB = tab2[lr % 2], TAB2[lr % 2]
                    dma("sp", "tab%d" % (lr % 2), tab[:], btab[2 * o0 + lr + 8, hp], [], TAB)
                    for hh in range(2):
                        pl = hh * 64
                        s_sb, p_bf, pT, sm = s_sb2[hh], p_bf2[hh], pT2[hh], sm2[hh]
                        S, PBF, PT, SM = S2[hh], PBF2[hh], PT2[hh], SM2[hh]
                        sbanks = (4, 5) if hh == 0 else (0, 1)
                        tbank = 6 if hh == 0 else 2
                        vbank = 7 if hh == 0 else 3
                        for half in range(2):
                            b = sbanks[half]
                            pe_mm(P[b][0:64, :], QT[pl:pl + 64, lr * 64:(lr + 1) * 64],
                                  KT[pl:pl + 64, kb * 64 + half * 512: kb * 64 + (half + 1) * 512],
                                  True, True, [QTB, KTB], [PB[b]])
                            stt("dve", s_sb[:, half * 512:(half + 1) * 512], P[b][0:64, :], 0.125,
                                tab[:, hh * 1024 + half * 512: hh * 1024 + (half + 1) * 512], ALU.mult, ALU.add,
                                [PB[b]] + TAB, [S])
                        fw.op("dve", (lambda sm, s_sb: lambda e: e.reduce_max(out=sm[:, 0:1], in_=s_sb[:], axis=AX.X))(sm, s_sb), [S], [SM])
                        ts("dve", sm[:, 1:2], sm[:, 0:1], -1.0, None, ALU.mult, None, [SM], [SM])
                        act(p_bf[:], s_sb[:], AF.Exp, [S, SM], [PBF, SM], bias=sm[:, 1:2], accum_out=sm[:, 2:3])
                        pTp = P[tbank][:].bitcast(BF16)
                        for c in range(8):
                            pe_tr(pTp[:, c * 64:(c + 1) * 64], p_bf[:, c * 128:(c + 1) * 128], identb[0:64, 0:64],
                                  [PBF, CONST], [PB[tbank]])
                        cp("act", pT[:], pTp[:, 0:512].rearrange("p (c q) -> p c q", c=8), [PB[tbank]], [PT])
                        for c in range(8):
                            pe_mm(P[vbank][0:64, hh * 64:(hh + 1) * 64], pT[:, c, :], V[:, kb // 2 + c, hh * 64:(hh + 1) * 64],
                                  c == 0, c == 7, [PT, VB], [PB[vbank]])
                        fw.op("dve", (lambda sm: lambda e: e.reciprocal(out=sm[:, 3:4], in_=sm[:, 2:3]))(sm), [SM], [SM])
                        ts("dve", o_sb[:, hh * 64:(hh + 1) * 64], P[vbank][0:64, hh * 64:(hh + 1) * 64], sm[:, 3:4], None,
                           ALU.mult, None, [PB[vbank], SM], [OSB])
                    pe_tr(P[3][:, 0:64], o_sb[:], ident[0:64, 0:64], [OSB, CONST], [PB[3]])
                    cp("act", aoT[:, hp, lr * 64:(lr + 1) * 64], P[3][:, 0:64], [PB[3]], [AOT])
            fw.barrier()
            m1.close()
            stage2 = sb(mx, "a_stage2", [128, 8, 256], F32)
            STG2 = Buf()
            wo_bf = sb(mx, "a_wo", [128, 8, 1024], BF16)
            WO = Buf()
            load_w_bf(w_o, 0, 1024, wo_bf, 0, WO, stage2, STG2, "wst2")
            for t in range(n):
                for half in range(2):
                    b = half
                    for k in range(8):
                        pe_mm(P[b][:], aoT[:, k, t * 128:(t + 1) * 128], wo_bf[:, k, half * 512:(half + 1) * 512],
                              k == 0, k == 7, [AOT, WO], [PB[b]])
                    tt("dve", h_sb[:, t, half * 512:(half + 1) * 512], h_sb[:, t, half * 512:(half + 1) * 512],
                       P[b][:], ALU.add, [HB[t], PB[b]], [HB[t]])
            fw.barrier()
            mx.close()

        pz = contextlib.ExitStack()
        hnT = sb(pz, "p_hnT", [128, 8, 2048], BF16)
        HNT = [Buf() for _ in range(NT)]
        p0 = contextlib.ExitStack()
        gain = sb(p0, "p_gain", [128, 1024], F32)
        GB = Buf()
        junk = sb(p0, "p_junk", [128, 1024], BF16)
        st = sb(p0, "p_st", [128, 4], F32)
        hn = sb(p0, "p_hn", [128, 1024], F32)
        JB, SB_, HN = Buf(), Buf(), Buf()
        dma("sp", "gain", gain[:], g_ffn, [], [GB])
        for t in range(n):
            rmsnorm((junk, JB, st, SB_), h_sb[:, t, :], [HB[t]], gain, GB, hn[:], HN)
            to_featmajor(hn, HN, hnT, t * 128, HNT[t])
        fw.barrier()
        p0.close()

        p1 = contextlib.ExitStack()
        stage = sb(p1, "p_stage", [128, 8, 256], F32)
        STG = Buf()
        wq_bf = sb(p1, "p_wq", [128, 8, 2048], BF16)
        WQ = Buf()
        load_w_bf(wq, 0, 2048, wq_bf, 0, WQ, stage, STG, "wst")
        kT_bf = sb(p1, "p_kT", [128, 16, 128], BF16)
        KTB = Buf()
        dma("sp", "wst", stage[:].rearrange("p k n -> p (k n)"), keysT, [], [STG])
        cp("pool", kT_bf[:].rearrange("p c n -> p (c n)"), stage[:].rearrange("p k n -> p (k n)"), [STG], [KTB])
        qT_sb = sb(p1, "p_qT", [128, 16, 128], BF16)
        sc_sb = sb(p1, "p_sc", [128, 16, 128], F32)
        sc2 = sb(p1, "p_sc2", [128, 128], F32)
        sv = sb(p1, "p_sv", [128, 16, 16], F32)
        si = sb(p1, "p_si", [128, 16, 16], U32)
        cand = sb(p1, "p_cand", [128, 8, 256], F32)
        cand2 = sb(p1, "p_cand2", [128, 256], F32)
        best = sb(p1, "p_best", [128, 8, 16], F32)
        pos = sb(p1, "p_pos", [128, 8, 16], U32)
        gex = sb(p1, "p_gex", [128, 8, 16], F32)
        gz = sb(p1, "p_gz", [128, 16], F32)
        gate = sb(p1, "p_gate", [128, 8, 16], F32)
        au = sb(p1, "p_au", [128, 8, 16], U32)
        bu = sb(p1, "p_bu", [128, 8, 16], U32)
        af = sb(p1, "p_af", [128, 8, 16], F32)
        bf = sb(p1, "p_bf", [128, 8, 16], F32)
        sif = sb(p1, "p_sif", [128, 16, 16], F32)
        i_f = sb(p1, "p_if", [128, 128], F32)
        j_f = sb(p1, "p_jf", [128, 128], F32)
        iT = sb(p1, "p_iT", [128, 128], BF16)
        jT = sb(p1, "p_jT", [128, 128], BF16)
        gT = sb(p1, "p_gT", [128, 128], BF16)
        A_s = sb(p1, "p_A", [128, 16, 128], BF16)
        B_s = sb(p1, "p_B", [128, 16, 128], BF16)
        G_sb = sb(p1, "p_G", [128, 128, 64], BF16)
        QTB = [Buf() for _ in range(4)]
        SCB = [Buf() for _ in range(4)]
        R = Buf()
        TR = Buf()
        AB, BB, GSB = Buf(), Buf(), Buf()
        GSCR = [[Buf(), Buf()] for _ in range(NT)]
        svv = sv[:].rearrange("p (h two) k -> p h two k", two=2)
        siv = sif[:].rearrange("p (h two) k -> p h two k", two=2)
        iot16 = iot[:, 0:16].unsqueeze(1).unsqueeze(1).to_broadcast([128, 8, 16, 16])
        iot_b = iotb[:, :].unsqueeze(1).to_broadcast([128, 16, 128])
        cand4 = cand[:].rearrange("p h (a b) -> p h a b", a=16)
        for t in range(n):
            for cb in range(4):
                for cc in range(4):
                    c = cb * 4 + cc
                    for k in range(8):
                        pe_mm(P[cb][:, cc * 128:(cc + 1) * 128], wq_bf[:, k, c * 128:(c + 1) * 128],
                              hnT[:, k, t * 128:(t + 1) * 128], k == 0, k == 7, [WQ, HNT[t]], [PB[cb]])
                cp("act", qT_sb[:, cb * 4:(cb + 1) * 4, :], P[cb][:].rearrange("p (c n) -> p c n", c=4), [PB[cb]], [QTB[cb]])
            for cb in range(4):
                for cc in range(4):
                    c = cb * 4 + cc
                    pe_mm(P[4 + cb][:, cc * 128:(cc + 1) * 128], qT_sb[:, c, :], kT_bf[:, c, :], True, True,
                          [QTB[cb], KTB], [PB[4 + cb]])
                cp("act", sc_sb[:, cb * 4:(cb + 1) * 4, :], P[4 + cb][:].rearrange("p (c n) -> p c n", c=4),
                   [PB[4 + cb]], [SCB[cb]])

            def top16(vals_ap, scratch_ap, out_v, out_i, rbufs):
                fw.op("dve", lambda e: e.max(out=out_v[:, 0:8], in_=vals_ap), rbufs + [R], [R])
                fw.op("dve", lambda e: e.max_index(out=out_i[:, 0:8], in_max=out_v[:, 0:8], in_values=vals_ap), rbufs + [R], [R])
                fw.op("dve", lambda e: e.match_replace(out=scratch_ap, in_to_replace=out_v[:, 0:8], in_values=vals_ap,
                                                       imm_value=NEG), rbufs + [R], [R])
                fw.op("dve", lambda e: e.max(out=out_v[:, 8:16], in_=scratch_ap), [R], [R])
                fw.op("dve", lambda e: e.max_index(out=out_i[:, 8:16], in_max=out_v[:, 8:16], in_values=scratch_ap), [R], [R])

            for c in range(16):
                top16(sc_sb[:, c, :], sc2[:], sv[:, c, :], si[:, c, :], [SCB[c // 4]])
            tt("dve", cand4, svv[:, :, 0, :].unsqueeze(3).to_broadcast([128, 8, 16, 16]),
               svv[:, :, 1, :].unsqueeze(2).to_broadcast([128, 8, 16, 16]), ALU.add, [R], [R])
            for h in range(8):
                top16(cand[:, h, :], cand2[:], best[:, h, :], pos[:, h, :], [])
            tt("dve", gex[:], best[:], best[:, :, 0:1].to_broadcast([128, 8, 16]), ALU.subtract, [R], [R])
            act(gex[:], gex[:], AF.Exp, [R], [R])
            fw.op("dve", lambda e: e.reduce_sum(out=gz[:, 0:8], in_=gex[:], axis=AX.X), [R], [R])
            fw.op("dve", lambda e: e.reciprocal(out=gz[:, 8:16], in_=gz[:, 0:8]), [R], [R])
            tt("dve", gate[:], gex[:], gz[:, 8:16].unsqueeze(2).to_broadcast([128, 8, 16]), ALU.mult, [R], [R])
            fw.op("dve", lambda e: e.tensor_single_scalar(out=au[:], in_=pos[:], scalar=4, op=ALU.logical_shift_right), [R], [R])
            fw.op("dve", lambda e: e.tensor_single_scalar(out=bu[:], in_=pos[:], scalar=15, op=ALU.bitwise_and), [R], [R])
            cp("dve", af[:], au[:], [R], [R])
            cp("dve", bf[:], bu[:], [R], [R])
            cp("dve", sif[:], si[:], [R], [R])
            for (xf, half, dsti) in ((af, 0, i_f), (bf, 1, j_f)):
                tt("dve", cand4, xf[:].unsqueeze(3).to_broadcast([128, 8, 16, 16]), iot16, ALU.is_equal, [R, CONST], [R])
                tt("dve", cand4, cand4, siv[:, :, half, :].unsqueeze(2).to_broadcast([128, 8, 16, 16]), ALU.mult, [R], [R])
                fw.op("dve", (lambda d_: lambda e: e.tensor_reduce(
                    out=d_[:], in_=cand[:].rearrange("p h (k a) -> p (h k) a", a=16), axis=AX.X, op=ALU.add))(dsti), [R], [R])
            for (srci, dstT, b) in ((i_f[:], iT, 0), (j_f[:], jT, 1), (gate[:].rearrange("p h k -> p (h k)"), gT, 2)):
                pe_tr(P[b][:, 0:128], srci, ident[:], [R, CONST], [PB[b]])
                cp("act", dstT[:], P[b][:, 0:128], [PB[b]], [TR])
            for hb in range(2):
                for sbk in range(4):
                    t0 = hb * 64 + sbk * 16
                    tt("dve", A_s[:], iot_b, iT[:, t0:t0 + 16].unsqueeze(2).to_broadcast([128, 16, 128]), ALU.is_equal,
                       [TR, CONST], [AB])
                    tt("dve", A_s[:], A_s[:], gT[:, t0:t0 + 16].unsqueeze(2).to_broadcast([128, 16, 128]), ALU.mult,
                       [TR, AB], [AB])
                    tt("dve", B_s[:], iot_b, jT[:, t0:t0 + 16].unsqueeze(2).to_broadcast([128, 16, 128]), ALU.is_equal,
                       [TR, CONST], [BB])
                    for q in range(4):
                        b = q
                        for x in range(4):
                            tl = q * 4 + x
                            pe_mm(P[b][:, x * 128:(x + 1) * 128], B_s[:, tl, :], A_s[:, tl, :], True, True, [AB, BB], [PB[b]])
                        tk = sbk * 16 + q * 4
                        cp("act", G_sb[:, :, tk:tk + 4].rearrange("p n t -> p t n"),
                           P[b][:].rearrange("p (t n) -> p t n", t=4), [PB[b]], [GSB])
                dma("pool", "gst", gscr[t, hb], G_sb[:].rearrange("p n t -> p (n t)"), [GSB], [GSCR[t][hb]])
        fw.barrier()
        p1.close()

        p2 = contextlib.ExitStack()
        stU = sb(p2, "e_stU", [128, 8, 512], F32)
        stD = sb(p2, "e_stD", [128, 4, 1024], F32)
        wu = [sb(p2, "e_wu%d" % i, [128, 8, 512], BF16) for i in range(2)]
        wd = [sb(p2, "e_wd%d" % i, [128, 4, 1024], BF16) for i in range(2)]
        gsl = [sb(p2, "e_gs%d" % i, [128, 2, 4, 64], BF16) for i in range(3)]
        ge = [sb(p2, "e_ge%d" % i, [128, 4, 128], F32) for i in range(3)]
        ab = [sb(p2, "e_ab%d" % i, [128, 4, 128], BF16) for i in range(3)]
        STU, STD = Buf(), Buf()
        WU, WD = [Buf(), Buf()], [Buf(), Buf()]
        GSL, GE, ABB = [Buf() for _ in range(3)], [Buf() for _ in range(3)], [Buf() for _ in range(3)]
        wupv = wupT.rearrange("(k p) e -> p k e", p=128)
        wdnv = wdown.rearrange("(n q) d -> q n d", q=128)
        NG = 32
        its = [(g, t) for g in range(NG) for t in range(n)]

        def load_group_dma(g):
            dma("sp", "wu", stU[:], wupv[:, :, g * 512:(g + 1) * 512], [], [STU])
            dma("sp", "wd", stD[:], wdnv[:, g * 4:(g + 1) * 4, :], [], [STD])

        def load_group_cast(g):
            wpar = g % 2
            cp("act", wu[wpar][:], stU[:], [STU], [WU[wpar]])
            cp("pool", wd[wpar][:], stD[:], [STD], [WD[wpar]])

        load_group_dma(0)
        load_group_cast(0)

        def stage_up(k):
            g, t = its[k]
            r3, wpar = k % 3, g % 2
            if t == 0 and g + 1 < NG:
                load_group_dma(g + 1)
            if t == 2 and g + 1 < NG:
                load_group_cast(g + 1)
            for hb in range(2):
                dma("sp", "gsl%d" % r3, gsl[r3][:, hb].rearrange("p n t -> p (n t)"),
                    gscr[t, hb][:, g * 256:(g + 1) * 256], [GSCR[t][hb]], [GSL[r3]])
            for nn in range(4):
                for k8 in range(8):
                    pe_mm(P[r3][:, nn * 128:(nn + 1) * 128], wu[wpar][:, k8, nn * 128:(nn + 1) * 128],
                          hnT[:, k8, t * 128:(t + 1) * 128], k8 == 0, k8 == 7, [WU[wpar], HNT[t]], [PB[r3]])

        def stage_mid(k):
            r3 = k % 3
            act(ge[r3][:], P[r3][:].rearrange("p (n t) -> p n t", n=4), AF.Gelu, [PB[r3]], [GE[r3]])
            tt("dve", ab[r3][:].rearrange("p n (h t) -> p n h t", h=2),
               ge[r3][:].rearrange("p n (h t) -> p n h t", h=2),
               gsl[r3][:].rearrange("p h n t -> p n h t"), ALU.mult, [GE[r3], GSL[r3]], [ABB[r3]])

        def stage_down(k):
            g, t = its[k]
            r3, par, wpar = k % 3, k % 2, g % 2
            for nn in range(4):
                for half in range(2):
                    b = 3 + par * 2 + half
                    pe_mm(P[b][:], ab[r3][:, nn, :], wd[wpar][:, nn, half * 512:(half + 1) * 512], nn == 0, nn == 3,
                          [ABB[r3], WD[wpar]], [PB[b]])
            for half in range(2):
                b = 3 + par * 2 + half
                tt("dve", h_sb[:, t, half * 512:(half + 1) * 512], h_sb[:, t, half * 512:(half + 1) * 512], P[b][:],
                   ALU.add, [HB[t], PB[b]], [HB[t]])

        NI = len(its)
        for k in range(NI + 2):
            if k < NI:
                stage_up(k)
            if 1 <= k < NI + 1:
                stage_mid(k - 1)
            if k >= 2:
                stage_down(k - 2)
        fw.barrier()
        p2.close()

        pf = contextlib.ExitStack()
        ob = [sb(pf, "f_o%d" % i, [128, 1024], F32) for i in range(2)]
        OB = [Buf(), Buf()]
        if final:
            gain = sb(pf, "f_gain", [128, 1024], F32)
            GB = Buf()
            junk = sb(pf, "f_junk", [128, 1024], BF16)
            st = sb(pf, "f_st", [128, 4], F32)
            JB, SB_ = Buf(), Buf()
            dma("sp", "gain", gain[:], g_fin, [], [GB])
            for t in range(n):
                rmsnorm((junk, JB, st, SB_), h_sb[:, t, :], [HB[t]], gain, GB, ob[t % 2][:], OB[t % 2])
                dma("sp", "out%d" % (t % 2), hout[(o0 + t) * 128:(o0 + t + 1) * 128, :], ob[t % 2][:], [OB[t % 2]], [])
        else:
            for t in range(n):
                at = o0 + XOFF + t
                ts("dve", ob[t % 2][:], h_sb[:, t, :], vmask[:, at:at + 1], None, ALU.mult, None, [HB[t], VM], [OB[t % 2]])
                dma("sp", "out%d" % (t % 2), dst[at * 128:(at + 1) * 128, :], ob[t % 2][:], [OB[t % 2]], [])
        fw.barrier()
        pf.close()
        pz.close()

    vmask = sb(top, 'vmask', [128, XT], F32)
    VM = Buf()
    dma('sp', 'vm', vmask[:], valid, [], [VM])
    zs = contextlib.ExitStack()
    zt = sb(zs, 'zt', [128, 1024], F32)
    ZB = Buf()
    fw.op('dve', lambda e: e.memset(zt[:], 0.0), [], [ZB])
    for d_ in range(2):
        for t in range(XT):
            dma('sp', 'z%d' % (t % 2), HD[d_][t * 128:(t + 1) * 128, :], zt[:], [ZB], [])
    fw.barrier()
    zs.close()
    src = hx
    for li in range(4):
        kind = 'attn' if li % 2 == 0 else 'conv'
        dst = HD[li % 2]
        for (o0, n) in CHUNKS[li]:
            emit_chunk(li, kind, src, dst, o0, n, li == 3)
        src = dst
    fw.finalize()
    top.close()
    return nc


_PROG = []


def _ext(h, c):
    b, q = c // 4, c % 4
    hb = h[b].reshape(128, 64, 1024)
    out = np.zeros((XT * 2, 64, 1024), np.float32)
    r0 = 32 * q - 2 * XOFF
    lo, hi = max(r0, 0), min(r0 + XT * 2, 128)
    out[lo - r0:hi - r0] = hb[lo:hi]
    return out.reshape(XT * 128, 1024)


def _btab(rel_bias, q):
    qc = np.arange(64)[:, None]
    kc = np.arange(64)[None, :]
    ws = np.clip(qc - 8, 0, 48)
    vcol = (kc >= ws) & (kc < ws + 16)
    coff = np.clip(kc - qc + 15, 0, 30)
    out = np.full((TROWS, 16, 64, 16, 64), NEG, np.float32)
    for idx in range(TROWS):
        rr = idx - 8
        r = 32 * q + rr
        if r < 0 or r >= 128:
            continue
        kb = rr - 8 if rr % 2 == 0 else rr - 9
        kabs = 32 * q + kb
        rs = min(max(r - 4, 0), 120)
        for wr in range(16):
            krow = kabs + wr
            if rs <= krow < rs + 8:
                roff = krow - r + 7
                vals = rel_bias[:, roff, :][:, coff]
                out[idx, :, :, wr, :] = np.where(vcol[None], vals, np.float32(NEG))
    out = out.reshape(TROWS, 8, 2, 64, 1024).transpose(0, 1, 3, 2, 4).reshape(TROWS, 8, 64, 2048)
    return np.ascontiguousarray(out)


def _rep(v):
    return np.ascontiguousarray(np.broadcast_to(np.asarray(v, np.float32)[None, :], (128, 1024)))


def kernel(**inp):
    if not _PROG:
        _PROG.append(build_fused())
    nc = _PROG[0]
    h = np.ascontiguousarray(np.asarray(inp["x"], np.float32))
    common = {"g_fin": _rep(inp["norm_final"])}
    tabs = {}
    for i in range(4):
        j = i // 2
        common["g_mix%d" % i] = _rep(inp["norm_mix"][i])
        common["g_ffn%d" % i] = _rep(inp["norm_ffn"][i])
        common["wq%d" % i] = np.ascontiguousarray(inp["peer_w_query"][i], np.float32)
        common["keysT%d" % i] = np.ascontiguousarray(
            np.asarray(inp["peer_sub_keys"][i], np.float32).reshape(16, 128, 128).transpose(2, 0, 1).reshape(128, 2048))
        common["wupT%d" % i] = np.ascontiguousarray(np.asarray(inp["peer_w_up"][i], np.float32).T)
        common["wdown%d" % i] = np.ascontiguousarray(inp["peer_w_down"][i], np.float32)
        if i % 2 == 1:
            common["w_in%d" % i] = np.ascontiguousarray(inp["conv_w_in"][j], np.float32)
            common["w_cv%d" % i] = np.ascontiguousarray(
                np.asarray(inp["conv_w_conv"][j], np.float32).reshape(3, 8, 128).transpose(2, 0, 1).reshape(128, 24))
            common["w_out%d" % i] = np.ascontiguousarray(inp["conv_w_out"][j], np.float32)
        else:
            common["w_qkv%d" % i] = np.ascontiguousarray(inp["attn_w_qkv"][j], np.float32)
            common["w_o%d" % i] = np.ascontiguousarray(inp["attn_w_o"][j], np.float32)
            rb = np.asarray(inp["attn_rel_bias"][j], np.float32)
            tabs[i] = [_btab(rb, q) for q in range(4)]
    in_maps = []
    for c in range(8):
        q = c % 4
        m = dict(common)
        m["hx"] = _ext(h, c)
        v = np.zeros((XT,), np.float32)
        for t in range(XT):
            at = 16 * q + t - XOFF
            v[t] = 1.0 if 0 <= at < 64 else 0.0
        m["valid"] = np.ascontiguousarray(np.broadcast_to(v[None, :], (128, XT)))
        for i in (0, 2):
            m["btab%d" % i] = tabs[i][q]
        in_maps.append(m)
    res = run_bass_kernel_spmd(nc, in_maps, core_ids=list(range(8)))
    out = np.empty((2, 8192, 1024), np.float32)
    for c in range(8):
        b, q = c // 4, c % 4
        out[b, q * 2048:(q + 1) * 2048] = np.asarray(res.results[c]["hout"], np.float32)
    return out
```

```python
import contextlib
import numpy as np
import concourse.bass as bass
import concourse.mybir as mybir
from concourse.bass_utils import run_bass_kernel_spmd

F32 = mybir.dt.float32
BF16 = mybir.dt.bfloat16
U32 = mybir.dt.uint32
ALU = mybir.AluOpType
AF = mybir.ActivationFunctionType
AX = mybir.AxisListType

HALO = 8
EXT_ROWS = 32 + 2 * HALO
EXT_TOK = EXT_ROWS * 64
OWN0 = HALO * 64
NT = 16
NEG = -1e30
XT = 32
XOFF = 8
TROWS = 48
CHUNKS = [[(-4, 16), (12, 8)], [(-3, 16), (13, 6)], [(-2, 16), (14, 4)], [(0, 16)]]


class Buf:
    __slots__ = ("lw", "rd")

    def __init__(self):
        self.lw = None
        self.rd = []


class FW:
    ENGS = ("pe", "dve", "act", "pool", "sp")

    def __init__(self, nc):
        self.nc = nc
        self.ins = {e: [] for e in self.ENGS}
        self.dma_cnt = {}
        self.dma_keys = []

    def _deps(self, reads, writes):
        deps = []
        for b in reads:
            if b.lw is not None:
                deps.append(b.lw)
        for b in writes:
            if b.lw is not None:
                deps.append(b.lw)
            deps.extend(b.rd)
        return deps

    def op(self, eng, fn, reads=(), writes=()):
        deps = self._deps(reads, writes)
        idx = len(self.ins[eng])
        self.ins[eng].append([fn, deps, None])
        tok = ("e", eng, idx)
        for b in writes:
            b.lw = tok
            b.rd = []
        for b in reads:
            b.rd.append(tok)
        return tok

    def dma(self, eng, key, fn, reads=(), writes=()):
        deps = self._deps(reads, writes)
        if key not in self.dma_cnt:
            self.dma_cnt[key] = 0
            self.dma_keys.append(key)
        self.dma_cnt[key] += 16
        tok = ("d", key, self.dma_cnt[key])
        self.ins[eng].append([fn, deps, key])
        for b in writes:
            b.lw = tok
            b.rd = []
        for b in reads:
            b.rd.append(tok)
        return tok

    def barrier(self):
        toks = []
        for e in self.ENGS:
            for idx in range(len(self.ins[e]) - 1, -1, -1):
                fn, deps, key = self.ins[e][idx]
                if fn is not None and key is None:
                    toks.append(("e", e, idx))
                    break
        for k in self.dma_keys:
            toks.append(("d", k, self.dma_cnt[k]))
        for e in self.ENGS:
            self.ins[e].append([None, list(toks), None])

    def finalize(self):
        nc = self.nc
        needed = {e: set() for e in self.ENGS}
        for e in self.ENGS:
            for idx, (fn, deps, key) in enumerate(self.ins[e]):
                for d in deps:
                    if d[0] == "e" and (d[1] != e or key is not None or e != "pe"):
                        needed[d[1]].add(d[2])
        rank = {}
        for e in self.ENGS:
            r = 0
            for idx in range(len(self.ins[e])):
                if idx in needed[e]:
                    r += 1
                    rank[(e, idx)] = r
        es = contextlib.ExitStack()
        esem = {e: es.enter_context(nc.semaphore("s_" + e)) for e in self.ENGS}
        dsem = {k: es.enter_context(nc.semaphore("d_%d" % i)) for i, k in enumerate(self.dma_keys)}
        block = es.enter_context(nc.Block())
        ins = self.ins

        def make(e):
            def body(engine):
                waited = {}
                for idx, (fn, deps, key) in enumerate(ins[e]):
                    for d in deps:
                        if d[0] == "e":
                            if d[1] == e and key is None and e == "pe":
                                continue
                            wk = ("e", d[1])
                            val = rank[(d[1], d[2])]
                            sem = esem[d[1]]
                        else:
                            wk = ("d", d[1])
                            val = d[2]
                            sem = dsem[d[1]]
                        if waited.get(wk, 0) >= val:
                            continue
                        waited[wk] = val
                        engine.wait_ge(sem, val)
                    if fn is None:
                        continue
                    inst = fn(engine)
                    if key is not None:
                        inst.then_inc(dsem[key], 16)
                    elif idx in needed[e]:
                        inst.then_inc(esem[e], 1)
            return body

        block.tensor(make("pe"))
        block.vector(make("dve"))
        block.scalar(make("act"))
        block.gpsimd(make("pool"))
        block.sync(make("sp"))
        es.close()


def build_fused():
    nc = bass.Bass("TRN2", target_bir_lowering=False)
    fw = FW(nc)

    def din(name, shape, dt=F32):
        return nc.dram_tensor(name, shape, dt, kind="ExternalInput").ap()

    hx = din("hx", [XT * 128, 1024])
    valid = din("valid", [128, XT])
    g_fin = din("g_fin", [128, 1024])
    L = []
    for i in range(4):
        d = {"g_mix": din("g_mix%d" % i, [128, 1024]), "g_ffn": din("g_ffn%d" % i, [128, 1024]),
             "wq": din("wq%d" % i, [1024, 2048]), "keysT": din("keysT%d" % i, [128, 2048]),
             "wupT": din("wupT%d" % i, [1024, 16384]), "wdown": din("wdown%d" % i, [16384, 1024])}
        if i % 2 == 1:
            d["w_in"] = din("w_in%d" % i, [1024, 3072])
            d["w_cv"] = din("w_cv%d" % i, [128, 24])
            d["w_out"] = din("w_out%d" % i, [1024, 1024])
        else:
            d["w_qkv"] = din("w_qkv%d" % i, [1024, 3072])
            d["w_o"] = din("w_o%d" % i, [1024, 1024])
            d["btab"] = din("btab%d" % i, [TROWS, 8, 64, 2048])
        L.append(d)
    hout = nc.dram_tensor("hout", [2048, 1024], F32, kind="ExternalOutput").ap()
    gscr = nc.dram_tensor("gscr", [NT, 2, 128, 8192], BF16).ap()
    HD = [nc.dram_tensor("hd%d" % i, [XT * 128, 1024], F32).ap() for i in range(2)]

    top = contextlib.ExitStack()

    _cnt = [0]

    def sb(es, name, shape, dt):
        _cnt[0] += 1
        return es.enter_context(nc.sbuf_tensor("%s_%d" % (name, _cnt[0]), shape, dt))

    P = [top.enter_context(nc.psum_tensor("ps%d" % i, [128, 512], F32)) for i in range(8)]
    PB = [Buf() for _ in range(8)]

    h_sb = sb(top, "h_sb", [128, NT, 1024], F32)
    HB = [Buf() for _ in range(NT)]
    ident = sb(top, "ident", [128, 128], F32)
    identb = sb(top, "identb", [128, 128], BF16)
    iot = sb(top, "iot", [128, 128], F32)
    iotb = sb(top, "iotb", [128, 128], BF16)
    pidx = sb(top, "pidx", [128, 1], F32)
    CONST = Buf()

    def pe_mm(out, lhsT, rhs, start, stop, reads, writes):
        fw.op("pe", lambda e: e.matmul(out, lhsT, rhs, start=start, stop=stop), reads, writes)

    def pe_tr(out, in_, idn, reads, writes):
        fw.op("pe", lambda e: e.transpose(out=out, in_=in_, identity=idn), reads, writes)

    def tt(eng, out, in0, in1, op, reads, writes):
        fw.op(eng, lambda e: e.tensor_tensor(out=out, in0=in0, in1=in1, op=op), reads, writes)

    def ts(eng, out, in0, s1, s2, op0, op1, reads, writes):
        if op1 is None:
            fw.op(eng, lambda e: e.tensor_scalar(out=out, in0=in0, scalar1=s1, scalar2=None, op0=op0), reads, writes)
        else:
            fw.op(eng, lambda e: e.tensor_scalar(out=out, in0=in0, scalar1=s1, scalar2=s2, op0=op0, op1=op1), reads, writes)

    def stt(eng, out, in0, scalar, in1, op0, op1, reads, writes):
        fw.op(eng, lambda e: e.scalar_tensor_tensor(out=out, in0=in0, scalar=scalar, in1=in1, op0=op0, op1=op1), reads, writes)

    def cp(eng, out, in_, reads, writes):
        if eng == "act":
            fw.op(eng, lambda e: e.copy(out=out, in_=in_), reads, writes)
        else:
            fw.op(eng, lambda e: e.tensor_copy(out=out, in_=in_), reads, writes)

    def act(out, in_, func, reads, writes, bias=None, scale=None, accum_out=None):
        kw = {}
        if bias is not None:
            kw["bias"] = bias
        if scale is not None:
            kw["scale"] = scale
        if accum_out is not None:
            kw["accum_out"] = accum_out
        fw.op("act", lambda e: e.activation(out=out, in_=in_, func=func, **kw), reads, writes)

    def dma(eng, key, out, in_, reads, writes):
        fw.dma(eng, key, lambda e: e.dma_start(out=out, in_=in_), reads, writes)

    fw.op("pool", lambda e: e.iota(iot[:], pattern=[[1, 128]], base=0, channel_multiplier=0,
                                   allow_small_or_imprecise_dtypes=True), writes=[CONST])
    fw.op("pool", lambda e: e.iota(pidx[:], pattern=[[0, 1]], base=0, channel_multiplier=1,
                                   allow_small_or_imprecise_dtypes=True), writes=[CONST])
    ts("dve", ident[:], iot[:], pidx[:, 0:1], None, ALU.is_equal, None, [CONST], [CONST])
    cp("dve", identb[:], ident[:], [CONST], [CONST])
    cp("dve", iotb[:], iot[:], [CONST], [CONST])

    def rmsnorm(es_bufs, src_ap, src_bufs, gain_t, gain_buf, out_ap, out_buf):
        junk, JB, st, SB_ = es_bufs
        act(junk[:], src_ap, AF.Square, src_bufs, [JB, SB_], accum_out=st[:, 0:1])
        ts("dve", st[:, 1:2], st[:, 0:1], 1.0 / 1024.0, 1e-6, ALU.mult, ALU.add, [SB_], [SB_])
        act(st[:, 2:3], st[:, 1:2], AF.Sqrt, [SB_], [SB_])
        fw.op("dve", lambda e: e.reciprocal(out=st[:, 3:4], in_=st[:, 2:3]), [SB_], [SB_])
        stt("dve", out_ap, src_ap, st[:, 3:4], gain_t[:], ALU.mult, ALU.mult, src_bufs + [SB_, gain_buf], [out_buf])

    def to_featmajor(src, src_buf, dstT, col0, dst_buf, banks=(0, 1)):
        for half in range(2):
            b = banks[half]
            for kk in range(4):
                k = half * 4 + kk
                pe_tr(P[b][:, kk * 128:(kk + 1) * 128], src[:, k * 128:(k + 1) * 128], ident[:],
                      [src_buf, CONST], [PB[b]])
            cp("act", dstT[:, half * 4:(half + 1) * 4, col0:col0 + 128],
               P[b][:].rearrange("p (k t) -> p k t", k=4), [PB[b]], [dst_buf])

    def load_w_bf(w_ap, col0, ncols, dst, dst_col0, dst_buf, stage, stage_buf, key, piece=256):
        wv = w_ap.rearrange("(k p) n -> p k n", p=128)
        for c in range(0, ncols, piece):
            dma("sp", key, stage[:, :, 0:piece], wv[:, :, col0 + c: col0 + c + piece], [], [stage_buf])
            cp("pool", dst[:, :, dst_col0 + c: dst_col0 + c + piece], stage[:, :, 0:piece], [stage_buf], [dst_buf])

    def emit_chunk(li, kind, src, dst, o0, n, final):
        W = L[li]
        g_mix, g_ffn = W['g_mix'], W['g_ffn']
        wq, keysT, wupT, wdown = W['wq'], W['keysT'], W['wupT'], W['wdown']
        ET = (n + 8) * 128
        HB = [Buf() for _ in range(NT)]
        for t in range(n):
            dma('sp', 'h%d' % (t % 4), h_sb[:, t, :], src[(o0 + XOFF + t) * 128:(o0 + XOFF + t + 1) * 128, :], [], [HB[t]])
        if kind == "conv":
            mx = contextlib.ExitStack()
            hnT = sb(mx, "c_hnT", [128, 8, 2304], BF16)
            HNT = Buf()
            gain = sb(mx, "c_gain", [128, 1024], F32)
            GB = Buf()
            junk = sb(mx, "c_junk", [128, 1024], BF16)
            st = sb(mx, "c_st", [128, 4], F32)
            hn = sb(mx, "c_hn", [128, 1024], F32)
            xt = sb(mx, "c_xt", [128, 1024], F32)
            JB, SB_, HN, XTB = Buf(), Buf(), Buf(), Buf()
            dma("sp", "gain", gain[:], g_mix, [], [GB])
            w_in, w_cv, w_out = W["w_in"], W["w_cv"], W["w_out"]
            NU = (n + 2) * 128
            for i in range(n + 2):
                if 1 <= i < n + 1:
                    t = i - 1
                    tsrc, sbufs = h_sb[:, t, :], [HB[t]]
                else:
                    at = o0 + XOFF - 1 + i
                    dma("sp", "cx", xt[:], src[at * 128:(at + 1) * 128, :], [], [XTB])
                    tsrc, sbufs = xt[:], [XTB]
                rmsnorm((junk, JB, st, SB_), tsrc, sbufs, gain, GB, hn[:], HN)
                to_featmajor(hn, HN, hnT, i * 128, HNT)
            stage = sb(mx, "c_stage", [128, 8, 128], F32)
            STG = Buf()
            wcv = sb(mx, "c_wcv", [128, 24], F32)
            WCV = Buf()
            dma("sp", "wcv", wcv[:], w_cv, [], [WCV])
            zT = sb(mx, "c_zT", [128, 8, 2048], BF16)
            ZT = Buf()
            wtri = sb(mx, "c_wtri", [128, 8, 384], BF16)
            WTRI = Buf()
            u = sb(mx, "c_u", [128, 2304], F32)
            gbt = sb(mx, "c_gb", [128, 2304], F32)
            gct = sb(mx, "c_gc", [128, 384], F32)
            t1 = sb(mx, "c_t1", [128, 2048], F32)
            U, GBT, GCT, T1 = Buf(), Buf(), Buf(), Buf()
            for c in range(8):
                for j in range(3):
                    load_w_bf(w_in, j * 1024 + c * 128, 128, wtri, j * 128, WTRI, stage, STG, "wst", piece=128)
                for tc_ in range((n + 2) // 2):
                    c0 = tc_ * 256
                    for j in range(3):
                        b = (tc_ % 2) * 3 + j
                        for k in range(8):
                            pe_mm(P[b][:, 0:256], wtri[:, k, j * 128:(j + 1) * 128], hnT[:, k, c0:c0 + 256],
                                  k == 0, k == 7, [WTRI, HNT], [PB[b]])
                    b0 = (tc_ % 2) * 3
                    cp("act", gbt[:, c0:c0 + 256], P[b0][:, 0:256], [PB[b0]], [GBT])
                    cp("act", gct[:, 0:256], P[b0 + 1][:, 0:256], [PB[b0 + 1]], [GCT])
                    tt("dve", u[:, c0:c0 + 256], gct[:, 0:256], P[b0 + 2][:, 0:256], ALU.mult, [GCT, PB[b0 + 2]], [U])
                NO = n * 128
                ts("dve", t1[:, 0:NO], u[:, 128:128 + NO], wcv[:, 8 + c:9 + c], None, ALU.mult, None, [U, WCV], [T1])
                stt("dve", t1[:, 0:NO], u[:, 127:127 + NO], wcv[:, c:c + 1], t1[:, 0:NO], ALU.mult, ALU.add, [U, WCV, T1], [T1])
                stt("dve", t1[:, 0:NO], u[:, 129:129 + NO], wcv[:, 16 + c:17 + c], t1[:, 0:NO], ALU.mult, ALU.add, [U, WCV, T1], [T1])
                tt("dve", zT[:, c, 0:NO], t1[:, 0:NO], gbt[:, 128:128 + NO], ALU.mult, [T1, GBT], [ZT])
            wo_bf = sb(mx, "c_wo", [128, 8, 1024], BF16)
            WO = Buf()
            load_w_bf(w_out, 0, 1024, wo_bf, 0, WO, stage, STG, "wst", piece=128)
            for t in range(n):
                for half in range(2):
                    b = 6 + half
                    for k in range(8):
                        pe_mm(P[b][:], zT[:, k, t * 128:(t + 1) * 128], wo_bf[:, k, half * 512:(half + 1) * 512],
                              k == 0, k == 7, [ZT, WO], [PB[b]])
                    tt("dve", h_sb[:, t, half * 512:(half + 1) * 512], h_sb[:, t, half * 512:(half + 1) * 512],
                       P[b][:], ALU.add, [HB[t], PB[b]], [HB[t]])
            fw.barrier()
            mx.close()

        if kind == "attn":
            mx = contextlib.ExitStack()
            m1 = contextlib.ExitStack()
            aoT = sb(mx, "a_aoT", [128, 8, 2048], BF16)
            AOT = Buf()
            hnT = sb(m1, "a_hnT", [128, 8, EXT_TOK], BF16)
            HNT = Buf()
            m0 = contextlib.ExitStack()
            gain = sb(m0, "a_gain", [128, 1024], F32)
            GB = Buf()
            junk = sb(m0, "a_junk", [128, 1024], BF16)
            st = sb(m0, "a_st", [128, 4], F32)
            hn = sb(m0, "a_hn", [128, 1024], F32)
            xt = sb(m0, "a_xt", [128, 1024], F32)
            JB, SB_, HN, XTB = Buf(), Buf(), Buf(), Buf()
            dma("sp", "gain", gain[:], g_mix, [], [GB])
            w_qkv, w_o, btab = W["w_qkv"], W["w_o"], W["btab"]
            for i in range(n + 8):
                if 4 <= i < n + 4:
                    t = i - 4
                    tsrc, sbufs = h_sb[:, t, :], [HB[t]]
                else:
                    at = o0 + XOFF - 4 + i
                    dma("sp", "cx", xt[:], src[at * 128:(at + 1) * 128, :], [], [XTB])
                    tsrc, sbufs = xt[:], [XTB]
                rmsnorm((junk, JB, st, SB_), tsrc, sbufs, gain, GB, hn[:], HN)
                to_featmajor(hn, HN, hnT, i * 128, HNT)
            fw.barrier()
            m0.close()
            stage = sb(m1, "a_stage", [128, 8, 128], F32)
            STG = Buf()
            wtri = sb(m1, "a_wtri", [128, 8, 384], BF16)
            WTRI = Buf()
            QT = sb(m1, "a_QT", [128, 2048], BF16)
            KT = sb(m1, "a_KT", [128, EXT_TOK], BF16)
            V = sb(m1, "a_V", [128, EXT_TOK // 128, 128], BF16)
            QTB, KTB, VB = Buf(), Buf(), Buf()
            tab2 = [sb(m1, "a_tab%d" % i, [64, 2048], F32) for i in range(2)]
            TAB2 = [[Buf()], [Buf()]]
            s_sb2 = [sb(m1, "a_s%d" % i, [64, 1024], F32) for i in range(2)]
            p_bf2 = [sb(m1, "a_p%d" % i, [64, 1024], BF16) for i in range(2)]
            pT2 = [sb(m1, "a_pT%d" % i, [128, 8, 64], BF16) for i in range(2)]
            sm4 = [sb(m1, "a_sm%d" % i, [64, 8], F32) for i in range(4)]
            o_sb2 = [sb(m1, "a_o%d" % i, [64, 128], F32) for i in range(2)]
            S2, PBF2, PT2 = [Buf(), Buf()], [Buf(), Buf()], [Buf(), Buf()]
            SM4 = [Buf() for _ in range(4)]
            OSB2 = [Buf(), Buf()]
            for hp in range(8):
                for j in range(3):
                    load_w_bf(w_qkv, j * 1024 + hp * 128, 128, wtri, j * 128, WTRI, stage, STG, "wst", piece=128)
                for c4 in range(n // 4):
                    b = c4 % 2
                    for k in range(8):
                        pe_mm(P[b][:], wtri[:, k, 0:128], hnT[:, k, OWN0 + c4 * 512: OWN0 + (c4 + 1) * 512],
                              k == 0, k == 7, [WTRI, HNT], [PB[b]])
                    cp("act", QT[:, c4 * 512:(c4 + 1) * 512], P[b][:], [PB[b]], [QTB])
                for c6 in range((n + 8) // 4):
                    b = c6 % 2
                    for k in range(8):
                        pe_mm(P[b][:], wtri[:, k, 128:256], hnT[:, k, c6 * 512:(c6 + 1) * 512],
                              k == 0, k == 7, [WTRI, HNT], [PB[b]])
                    cp("act", KT[:, c6 * 512:(c6 + 1) * 512], P[b][:], [PB[b]], [KTB])
                for vt in range(n + 8):
                    b = 2 + (vt // 4) % 2
                    q4 = vt % 4
                    for k in range(8):
                        pe_mm(P[b][:, q4 * 128:(q4 + 1) * 128], hnT[:, k, vt * 128:(vt + 1) * 128], wtri[:, k, 256:384],
                              k == 0, k == 7, [WTRI, HNT], [PB[b]])
                    if q4 == 3:
                        cp("act", V[:, vt - 3:vt + 1, :], P[b][:].rearrange("p (a n) -> p a n", a=4), [PB[b]], [VB])
                def st_a(k):
                    lr, hh = k // 2, k % 2
                    e_ = HALO + lr
                    kb = e_ - 8 if e_ % 2 == 0 else e_ - 9
                    tab, TAB = tab2[lr % 2], TAB2[lr % 2]
                    if hh == 0:
                        dma("sp", "tab%d" % (lr % 2), tab[:], btab[2 * o0 + lr + 8, hp], [], TAB)
                    pl = hh * 64
                    s_sb, p_bf, sm = s_sb2[hh], p_bf2[hh], sm4[k % 4]
                    S, PBF, SM = S2[hh], PBF2[hh], SM4[k % 4]
                    sbanks = (4, 5) if hh == 0 else (0, 1)
                    for half in range(2):
                        b = sbanks[half]
                        pe_mm(P[b][0:64, :], QT[pl:pl + 64, lr * 64:(lr + 1) * 64],
                              KT[pl:pl + 64, kb * 64 + half * 512: kb * 64 + (half + 1) * 512],
                              True, True, [QTB, KTB], [PB[b]])
                        stt("dve", s_sb[:, half * 512:(half + 1) * 512], P[b][0:64, :], 0.125,
                            tab[:, hh * 1024 + half * 512: hh * 1024 + (half + 1) * 512], ALU.mult, ALU.add,
                            [PB[b]] + TAB, [S])
                    fw.op("dve", (lambda sm, s_sb: lambda e: e.reduce_max(out=sm[:, 0:1], in_=s_sb[:], axis=AX.X))(sm, s_sb), [S], [SM])
                    ts("dve", sm[:, 1:2], sm[:, 0:1], -1.0, None, ALU.mult, None, [SM], [SM])
                    act(p_bf[:], s_sb[:], AF.Exp, [S, SM], [PBF, SM], bias=sm[:, 1:2], accum_out=sm[:, 2:3])

                def st_b(k):
                    hh = k % 2
                    p_bf, pT = p_bf2[hh], pT2[hh]
                    PBF, PT = PBF2[hh], PT2[hh]
                    tbank = 6 if hh == 0 else 2
                    pTp = P[tbank][:].bitcast(BF16)
                    for c in range(8):
                        pe_tr(pTp[:, c * 64:(c + 1) * 64], p_bf[:, c * 128:(c + 1) * 128], identb[0:64, 0:64],
                              [PBF, CONST], [PB[tbank]])
                    cp("act", pT[:], pTp[:, 0:512].rearrange("p (c q) -> p c q", c=8), [PB[tbank]], [PT])

                def st_c(k):
                    lr, hh = k // 2, k % 2
                    e_ = HALO + lr
                    kb = e_ - 8 if e_ % 2 == 0 else e_ - 9
                    pT, sm = pT2[hh], sm4[k % 4]
                    PT, SM = PT2[hh], SM4[k % 4]
                    o_sb, OSB = o_sb2[lr % 2], OSB2[lr % 2]
                    vbank = 7 if hh == 0 else 3
                    for c in range(8):
                        pe_mm(P[vbank][0:64, hh * 64:(hh + 1) * 64], pT[:, c, :], V[:, kb // 2 + c, hh * 64:(hh + 1) * 64],
                              c == 0, c == 7, [PT, VB], [PB[vbank]])
                    fw.op("dve", (lambda sm: lambda e: e.reciprocal(out=sm[:, 3:4], in_=sm[:, 2:3]))(sm), [SM], [SM])
                    ts("dve", o_sb[:, hh * 64:(hh + 1) * 64], P[vbank][0:64, hh * 64:(hh + 1) * 64], sm[:, 3:4], None,
                       ALU.mult, None, [PB[vbank], SM], [OSB])
                    if hh == 1:
                        pe_tr(P[3][:, 0:64], o_sb[:], ident[0:64, 0:64], [OSB, CONST], [PB[3]])
                        cp("act", aoT[:, hp, lr * 64:(lr + 1) * 64], P[3][:, 0:64], [PB[3]], [AOT])

                NK = 4 * n
                for k in range(NK + 2):
                    if k < NK:
                        st_a(k)
                    if 1 <= k < NK + 1:
                        st_b(k - 1)
                    if k >= 2:
                        st_c(k - 2)
            fw.barrier()
            m1.close()
            stage2 = sb(mx, "a_stage2", [128, 8, 256], F32)
            STG2 = Buf()
            wo_bf = sb(mx, "a_wo", [128, 8, 1024], BF16)
            WO = Buf()
            load_w_bf(w_o, 0, 1024, wo_bf, 0, WO, stage2, STG2, "wst2")
            for t in range(n):
                for half in range(2):
                    b = half
                    for k in range(8):
                        pe_mm(P[b][:], aoT[:, k, t * 128:(t + 1) * 128], wo_bf[:, k, half * 512:(half + 1) * 512],
                              k == 0, k == 7, [AOT, WO], [PB[b]])
                    tt("dve", h_sb[:, t, half * 512:(half + 1) * 512], h_sb[:, t, half * 512:(half + 1) * 512],
                       P[b][:], ALU.add, [HB[t], PB[b]], [HB[t]])
            fw.barrier()
            mx.close()

        pz = contextlib.ExitStack()
        hnT = sb(pz, "p_hnT", [128, 8, 2048], BF16)
        HNT = [Buf() for _ in range(NT)]
        p0 = contextlib.ExitStack()
        gain = sb(p0, "p_gain", [128, 1024], F32)
        GB = Buf()
        junk = sb(p0, "p_junk", [128, 1024], BF16)
        st = sb(p0, "p_st", [128, 4], F32)
        hn = sb(p0, "p_hn", [128, 1024], F32)
        JB, SB_, HN = Buf(), Buf(), Buf()
        dma("sp", "gain", gain[:], g_ffn, [], [GB])
        for t in range(n):
            rmsnorm((junk, JB, st, SB_), h_sb[:, t, :], [HB[t]], gain, GB, hn[:], HN)
            to_featmajor(hn, HN, hnT, t * 128, HNT[t])
        fw.barrier()
        p0.close()

        p1 = contextlib.ExitStack()
        wq_bf = sb(p1, "p_wq", [128, 8, 2048], BF16)
        WQ = Buf()
        kT_bf = sb(p1, "p_kT", [128, 16, 128], BF16)
        KTB = Buf()
        ps_ = contextlib.ExitStack()
        stage = sb(ps_, "p_stage", [128, 8, 256], F32)
        STG = Buf()
        load_w_bf(wq, 0, 2048, wq_bf, 0, WQ, stage, STG, "wst")
        dma("sp", "wst", stage[:].rearrange("p k n -> p (k n)"), keysT, [], [STG])
        cp("pool", kT_bf[:].rearrange("p c n -> p (c n)"), stage[:].rearrange("p k n -> p (k n)"), [STG], [KTB])
        fw.barrier()
        ps_.close()
        qT_sb = sb(p1, "p_qT", [128, 16, 128], BF16)
        sc_sb2 = [sb(p1, "p_sc%d" % i, [128, 16, 128], F32) for i in range(2)]
        sc2 = sb(p1, "p_sc2", [128, 128], F32)
        sv = sb(p1, "p_sv", [128, 16, 16], F32)
        si = sb(p1, "p_si", [128, 16, 16], U32)
        cand = sb(p1, "p_cand", [128, 8, 256], F32)
        cand2 = sb(p1, "p_cand2", [128, 256], F32)
        best = sb(p1, "p_best", [128, 8, 16], F32)
        pos = sb(p1, "p_pos", [128, 8, 16], U32)
        gex = sb(p1, "p_gex", [128, 8, 16], F32)
        gz = sb(p1, "p_gz", [128, 16], F32)
        gate = sb(p1, "p_gate", [128, 8, 16], F32)
        au = sb(p1, "p_au", [128, 8, 16], U32)
        bu = sb(p1, "p_bu", [128, 8, 16], U32)
        af = sb(p1, "p_af", [128, 8, 16], F32)
        bf = sb(p1, "p_bf", [128, 8, 16], F32)
        sif = sb(p1, "p_sif", [128, 16, 16], F32)
        i_f = sb(p1, "p_if", [128, 128], F32)
        j_f = sb(p1, "p_jf", [128, 128], F32)
        iT = sb(p1, "p_iT", [128, 128], F32)
        jT = sb(p1, "p_jT", [128, 128], F32)
        gT = sb(p1, "p_gT", [128, 128], F32)
        A_s2 = [sb(p1, "p_A%d" % i, [128, 16, 128], BF16) for i in range(2)]
        B_s2 = [sb(p1, "p_B%d" % i, [128, 16, 128], BF16) for i in range(2)]
        G_sb = sb(p1, "p_G", [128, 128, 64], BF16)
        QTB = [Buf() for _ in range(4)]
        SCB2 = [[Buf() for _ in range(4)] for _ in range(2)]
        R = Buf()
        TR = Buf()
        AB2 = [[Buf() for _ in range(16)] for _ in range(2)]
        BB2 = [[Buf() for _ in range(16)] for _ in range(2)]
        GSB = Buf()
        GSCR = [[Buf(), Buf()] for _ in range(NT)]
        svv = sv[:].rearrange("p (h two) k -> p h two k", two=2)
        siv = sif[:].rearrange("p (h two) k -> p h two k", two=2)
        iot16 = iot[:, 0:16].unsqueeze(1).unsqueeze(1).to_broadcast([128, 8, 16, 16])
        iot_b = iotb[:, :].unsqueeze(1).to_broadcast([128, 16, 128])
        cand4 = cand[:].rearrange("p h (a b) -> p h a b", a=16)
        def top16(vals_ap, scratch_ap, out_v, out_i, rbufs):
            fw.op("dve", lambda e: e.max(out=out_v[:, 0:8], in_=vals_ap), rbufs + [R], [R])
            fw.op("dve", lambda e: e.max_index(out=out_i[:, 0:8], in_max=out_v[:, 0:8], in_values=vals_ap), rbufs + [R], [R])
            fw.op("dve", lambda e: e.match_replace(out=scratch_ap, in_to_replace=out_v[:, 0:8], in_values=vals_ap,
                                                   imm_value=NEG), rbufs + [R], [R])
            fw.op("dve", lambda e: e.max(out=out_v[:, 8:16], in_=scratch_ap), [R], [R])
            fw.op("dve", lambda e: e.max_index(out=out_i[:, 8:16], in_max=out_v[:, 8:16], in_values=scratch_ap), [R], [R])

        def p1_front(t):
            sc_sb, SCB = sc_sb2[t % 2], SCB2[t % 2]
            for cb in range(4):
                for cc in range(4):
                    c = cb * 4 + cc
                    for k in range(8):
                        pe_mm(P[cb][:, cc * 128:(cc + 1) * 128], wq_bf[:, k, c * 128:(c + 1) * 128],
                              hnT[:, k, t * 128:(t + 1) * 128], k == 0, k == 7, [WQ, HNT[t]], [PB[cb]])
                cp("act", qT_sb[:, cb * 4:(cb + 1) * 4, :], P[cb][:].rearrange("p (c n) -> p c n", c=4), [PB[cb]], [QTB[cb]])
            for cb in range(4):
                for cc in range(4):
                    c = cb * 4 + cc
                    pe_mm(P[4 + cb][:, cc * 128:(cc + 1) * 128], qT_sb[:, c, :], kT_bf[:, c, :], True, True,
                          [QTB[cb], KTB], [PB[4 + cb]])
                cp("act", sc_sb[:, cb * 4:(cb + 1) * 4, :], P[4 + cb][:].rearrange("p (c n) -> p c n", c=4),
                   [PB[4 + cb]], [SCB[cb]])


        def p1_mid(t):
            sc_sb, SCB = sc_sb2[t % 2], SCB2[t % 2]
            for c in range(16):
                top16(sc_sb[:, c, :], sc2[:], sv[:, c, :], si[:, c, :], [SCB[c // 4]])
            tt("dve", cand4, svv[:, :, 0, :].unsqueeze(3).to_broadcast([128, 8, 16, 16]),
               svv[:, :, 1, :].unsqueeze(2).to_broadcast([128, 8, 16, 16]), ALU.add, [R], [R])
            for h in range(8):
                top16(cand[:, h, :], cand2[:], best[:, h, :], pos[:, h, :], [])
            tt("dve", gex[:], best[:], best[:, :, 0:1].to_broadcast([128, 8, 16]), ALU.subtract, [R], [R])
            act(gex[:], gex[:], AF.Exp, [R], [R])
            fw.op("dve", lambda e: e.reduce_sum(out=gz[:, 0:8], in_=gex[:], axis=AX.X), [R], [R])
            fw.op("dve", lambda e: e.reciprocal(out=gz[:, 8:16], in_=gz[:, 0:8]), [R], [R])
            tt("dve", gate[:], gex[:], gz[:, 8:16].unsqueeze(2).to_broadcast([128, 8, 16]), ALU.mult, [R], [R])
            fw.op("dve", lambda e: e.tensor_single_scalar(out=au[:], in_=pos[:], scalar=4, op=ALU.logical_shift_right), [R], [R])
            fw.op("dve", lambda e: e.tensor_single_scalar(out=bu[:], in_=pos[:], scalar=15, op=ALU.bitwise_and), [R], [R])
            cp("dve", af[:], au[:], [R], [R])
            cp("dve", bf[:], bu[:], [R], [R])
            cp("dve", sif[:], si[:], [R], [R])
            for (xf, half, dsti) in ((af, 0, i_f), (bf, 1, j_f)):
                tt("dve", cand4, xf[:].unsqueeze(3).to_broadcast([128, 8, 16, 16]), iot16, ALU.is_equal, [R, CONST], [R])
                tt("dve", cand4, cand4, siv[:, :, half, :].unsqueeze(2).to_broadcast([128, 8, 16, 16]), ALU.mult, [R], [R])
                fw.op("dve", (lambda d_: lambda e: e.tensor_reduce(
                    out=d_[:], in_=cand[:].rearrange("p h (k a) -> p (h k) a", a=16), axis=AX.X, op=ALU.add))(dsti), [R], [R])

        def p1_tr(t):
            for (srci, dstT, b) in ((i_f[:], iT, 0), (j_f[:], jT, 1), (gate[:].rearrange("p h k -> p (h k)"), gT, 2)):
                pe_tr(P[b][:, 0:128], srci, ident[:], [R, CONST], [PB[b]])
                cp("act", dstT[:], P[b][:, 0:128], [PB[b]], [TR])

        def p1_back(t):
            for hb in range(2):
                for sbk in range(4):
                    t0 = hb * 64 + sbk * 16
                    A_s, B_s, AB, BB = A_s2[sbk % 2], B_s2[sbk % 2], AB2[sbk % 2], BB2[sbk % 2]
                    for tl in range(16):
                        tok = t0 + tl
                        ts("dve", A_s[:, tl, :], iotb[:], iT[:, tok:tok + 1], gT[:, tok:tok + 1], ALU.is_equal, ALU.mult,
                           [TR, CONST], [AB[tl]])
                        ts("dve", B_s[:, tl, :], iotb[:], jT[:, tok:tok + 1], None, ALU.is_equal, None,
                           [TR, CONST], [BB[tl]])
                    for q in range(4):
                        b = q
                        for x in range(4):
                            tl = q * 4 + x
                            pe_mm(P[b][:, x * 128:(x + 1) * 128], B_s[:, tl, :], A_s[:, tl, :], True, True,
                                  [AB[tl], BB[tl]], [PB[b]])
                        tk = sbk * 16 + q * 4
                        cp("act", G_sb[:, :, tk:tk + 4].rearrange("p n t -> p t n"),
                           P[b][:].rearrange("p (t n) -> p t n", t=4), [PB[b]], [GSB])
                dma("pool", "gst", gscr[t, hb], G_sb[:].rearrange("p n t -> p (n t)"), [GSB], [GSCR[t][hb]])

        p1_front(0)
        for t in range(n):
            p1_mid(t)
            if t + 1 < n:
                p1_front(t + 1)
            p1_tr(t)
            p1_back(t)
        fw.barrier()
        p1.close()

        p2 = contextlib.ExitStack()
        stU = [sb(p2, "e_stU%d" % i, [128, 8, 256], F32) for i in range(2)]
        stD = [sb(p2, "e_stD%d" % i, [128, 2, 1024], F32) for i in range(2)]
        wu = [sb(p2, "e_wu%d" % i, [128, 8, 512], BF16) for i in range(2)]
        wd = [sb(p2, "e_wd%d" % i, [128, 4, 1024], BF16) for i in range(2)]
        gsl = [sb(p2, "e_gs%d" % i, [128, 2, 4, 64], BF16) for i in range(3)]
        ge = [sb(p2, "e_ge%d" % i, [128, 4, 128], F32) for i in range(3)]
        ab = [sb(p2, "e_ab%d" % i, [128, 4, 128], BF16) for i in range(3)]
        STU, STD = [Buf(), Buf()], [Buf(), Buf()]
        WU, WD = [Buf(), Buf()], [Buf(), Buf()]
        GSL, GE, ABB = [Buf() for _ in range(3)], [Buf() for _ in range(3)], [Buf() for _ in range(3)]
        wupv = wupT.rearrange("(k p) e -> p k e", p=128)
        wdnv = wdown.rearrange("(n q) d -> q n d", q=128)
        NG = 32
        its = [(g, t) for g in range(NG) for t in range(n)]

        def load_group_dma(g):
            for hf in range(2):
                dma("sp", "wu%d" % hf, stU[hf][:], wupv[:, :, g * 512 + hf * 256: g * 512 + (hf + 1) * 256], [], [STU[hf]])
                dma("sp", "wd%d" % hf, stD[hf][:], wdnv[:, g * 4 + hf * 2: g * 4 + (hf + 1) * 2, :], [], [STD[hf]])

        def load_group_cast(g):
            wpar = g % 2
            for hf in range(2):
                cp("act", wu[wpar][:, :, hf * 256:(hf + 1) * 256], stU[hf][:], [STU[hf]], [WU[wpar]])
            cp("act", wd[wpar][:, 0:2, :], stD[0][:], [STD[0]], [WD[wpar]])
            cp("pool", wd[wpar][:, 2:4, :], stD[1][:], [STD[1]], [WD[wpar]])

        load_group_dma(0)
        load_group_cast(0)

        def stage_up(k):
            g, t = its[k]
            r3, wpar = k % 3, g % 2
            if t == 0 and g + 1 < NG:
                load_group_dma(g + 1)
            for hb in range(2):
                dma("sp", "gsl%d" % r3, gsl[r3][:, hb].rearrange("p n t -> p (n t)"),
                    gscr[t, hb][:, g * 256:(g + 1) * 256], [GSCR[t][hb]], [GSL[r3]])
            for nn in range(4):
                for k8 in range(8):
                    pe_mm(P[r3][:, nn * 128:(nn + 1) * 128], wu[wpar][:, k8, nn * 128:(nn + 1) * 128],
                          hnT[:, k8, t * 128:(t + 1) * 128], k8 == 0, k8 == 7, [WU[wpar], HNT[t]], [PB[r3]])
            if t == n - 1 and g + 1 < NG:
                load_group_cast(g + 1)

        def stage_mid(k):
            r3 = k % 3
            act(ge[r3][:], P[r3][:].rearrange("p (n t) -> p n t", n=4), AF.Gelu, [PB[r3]], [GE[r3]])
            tt("dve", ab[r3][:].rearrange("p n (h t) -> p n h t", h=2),
               ge[r3][:].rearrange("p n (h t) -> p n h t", h=2),
               gsl[r3][:].rearrange("p h n t -> p n h t"), ALU.mult, [GE[r3], GSL[r3]], [ABB[r3]])

        def stage_down(k):
            g, t = its[k]
            r3, par, wpar = k % 3, k % 2, g % 2
            for nn in range(4):
                for half in range(2):
                    b = 3 + par * 2 + half
                    pe_mm(P[b][:], ab[r3][:, nn, :], wd[wpar][:, nn, half * 512:(half + 1) * 512], nn == 0, nn == 3,
                          [ABB[r3], WD[wpar]], [PB[b]])
            for half in range(2):
                b = 3 + par * 2 + half
                tt("dve", h_sb[:, t, half * 512:(half + 1) * 512], h_sb[:, t, half * 512:(half + 1) * 512], P[b][:],
                   ALU.add, [HB[t], PB[b]], [HB[t]])

        NI = len(its)
        for k in range(NI + 2):
            if k < NI:
                stage_up(k)
            if 1 <= k < NI + 1:
                stage_mid(k - 1)
            if k >= 2:
                stage_down(k - 2)
        fw.barrier()
        p2.close()

        pf = contextlib.ExitStack()
        ob = [sb(pf, "f_o%d" % i, [128, 1024], F32) for i in range(2)]
        OB = [Buf(), Buf()]
        if final:
            gain = sb(pf, "f_gain", [128, 1024], F32)
            GB = Buf()
            junk = sb(pf, "f_junk", [128, 1024], BF16)
            st = sb(pf, "f_st", [128, 4], F32)
            JB, SB_ = Buf(), Buf()
            dma("sp", "gain", gain[:], g_fin, [], [GB])
            for t in range(n):
                rmsnorm((junk, JB, st, SB_), h_sb[:, t, :], [HB[t]], gain, GB, ob[t % 2][:], OB[t % 2])
                dma("sp", "out%d" % (t % 2), hout[(o0 + t) * 128:(o0 + t + 1) * 128, :], ob[t % 2][:], [OB[t % 2]], [])
        else:
            for t in range(n):
                at = o0 + XOFF + t
                ts("dve", ob[t % 2][:], h_sb[:, t, :], vmask[:, at:at + 1], None, ALU.mult, None, [HB[t], VM], [OB[t % 2]])
                dma("sp", "out%d" % (t % 2), dst[at * 128:(at + 1) * 128, :], ob[t % 2][:], [OB[t % 2]], [])
        fw.barrier()
        pf.close()
        pz.close()

    vmask = sb(top, 'vmask', [128, XT], F32)
    VM = Buf()
    dma('sp', 'vm', vmask[:], valid, [], [VM])
    zs = contextlib.ExitStack()
    zt = sb(zs, 'zt', [128, 1024], F32)
    ZB = Buf()
    fw.op('dve', lambda e: e.memset(zt[:], 0.0), [], [ZB])
    for d_ in range(2):
        for t in range(XT):
            dma('sp', 'z%d' % (t % 2), HD[d_][t * 128:(t + 1) * 128, :], zt[:], [ZB], [])
    fw.barrier()
    zs.close()
    src = hx
    for li in range(4):
        kind = 'attn' if li % 2 == 0 else 'conv'
        dst = HD[li % 2]
        for (o0, n) in CHUNKS[li]:
            emit_chunk(li, kind, src, dst, o0, n, li == 3)
        src = dst
    fw.finalize()
    top.close()
    return nc


_PROG = []


def _ext(h, c):
    b, q = c // 4, c % 4
    hb = h[b].reshape(128, 64, 1024)
    out = np.zeros((XT * 2, 64, 1024), np.float32)
    r0 = 32 * q - 2 * XOFF
    lo, hi = max(r0, 0), min(r0 + XT * 2, 128)
    out[lo - r0:hi - r0] = hb[lo:hi]
    return out.reshape(XT * 128, 1024)


def _btab(rel_bias, q):
    qc = np.arange(64)[:, None]
    kc = np.arange(64)[None, :]
    ws = np.clip(qc - 8, 0, 48)
    vcol = (kc >= ws) & (kc < ws + 16)
    coff = np.clip(kc - qc + 15, 0, 30)
    out = np.full((TROWS, 16, 64, 16, 64), NEG, np.float32)
    for idx in range(TROWS):
        rr = idx - 8
        r = 32 * q + rr
        if r < 0 or r >= 128:
            continue
        kb = rr - 8 if rr % 2 == 0 else rr - 9
        kabs = 32 * q + kb
        rs = min(max(r - 4, 0), 120)
        for wr in range(16):
            krow = kabs + wr
            if rs <= krow < rs + 8:
                roff = krow - r + 7
                vals = rel_bias[:, roff, :][:, coff]
                out[idx, :, :, wr, :] = np.where(vcol[None], vals, np.float32(NEG))
    out = out.reshape(TROWS, 8, 2, 64, 1024).transpose(0, 1, 3, 2, 4).reshape(TROWS, 8, 64, 2048)
    return np.ascontiguousarray(out)


def _rep(v):
    return np.ascontiguousarray(np.broadcast_to(np.asarray(v, np.float32)[None, :], (128, 1024)))


def kernel(**inp):
    if not _PROG:
        _PROG.append(build_fused())
    nc = _PROG[0]
    h = np.ascontiguousarray(np.asarray(inp["x"], np.float32))
    common = {"g_fin": _rep(inp["norm_final"])}
    tabs = {}
    for i in range(4):
        j = i // 2
        common["g_mix%d" % i] = _rep(inp["norm_mix"][i])
        common["g_ffn%d" % i] = _rep(inp["norm_ffn"][i])
        common["wq%d" % i] = np.ascontiguousarray(inp["peer_w_query"][i], np.float32)
        common["keysT%d" % i] = np.ascontiguousarray(
            np.asarray(inp["peer_sub_keys"][i], np.float32).reshape(16, 128, 128).transpose(2, 0, 1).reshape(128, 2048))
        common["wupT%d" % i] = np.ascontiguousarray(np.asarray(inp["peer_w_up"][i], np.float32).T)
        common["wdown%d" % i] = np.ascontiguousarray(inp["peer_w_down"][i], np.float32)
        if i % 2 == 1:
            common["w_in%d" % i] = np.ascontiguousarray(inp["conv_w_in"][j], np.float32)
            common["w_cv%d" % i] = np.ascontiguousarray(
                np.asarray(inp["conv_w_conv"][j], np.float32).reshape(3, 8, 128).transpose(2, 0, 1).reshape(128, 24))
            common["w_out%d" % i] = np.ascontiguousarray(inp["conv_w_out"][j], np.float32)
        else:
            common["w_qkv%d" % i] = np.ascontiguousarray(inp["attn_w_qkv"][j], np.float32)
            common["w_o%d" % i] = np.ascontiguousarray(inp["attn_w_o"][j], np.float32)
            rb = np.asarray(inp["attn_rel_bias"][j], np.float32)
            tabs[i] = [_btab(rb, q) for q in range(4)]
    in_maps = []
    for c in range(8):
        q = c % 4
        m = dict(common)
        m["hx"] = _ext(h, c)
        v = np.zeros((XT,), np.float32)
        for t in range(XT):
            at = 16 * q + t - XOFF
            v[t] = 1.0 if 0 <= at < 64 else 0.0
        m["valid"] = np.ascontiguousarray(np.broadcast_to(v[None, :], (128, XT)))
        for i in (0, 2):
            m["btab%d" % i] = tabs[i][q]
        in_maps.append(m)
    res = run_bass_kernel_spmd(nc, in_maps, core_ids=list(range(8)))
    out = np.empty((2, 8192, 1024), np.float32)
    for c in range(8):
        b, q = c // 4, c % 4
        out[b, q * 2048:(q + 1) * 2048] = np.asarray(res.results[c]["hout"], np.float32)
    return out
```

```python
import contextlib
import numpy as np
import concourse.bass as bass
import concourse.mybir as mybir
from concourse.bass_utils import run_bass_kernel_spmd

F32 = mybir.dt.float32
BF16 = mybir.dt.bfloat16
U32 = mybir.dt.uint32
ALU = mybir.AluOpType
AF = mybir.ActivationFunctionType
AX = mybir.AxisListType

HALO = 8
EXT_ROWS = 32 + 2 * HALO
EXT_TOK = EXT_ROWS * 64
OWN0 = HALO * 64
NT = 16
NEG = -1e30
XT = 32
XOFF = 8
TROWS = 48
SPECIAL_ROWS = (0, 1, 2, 3, 29, 30, 31)


def _rowwin(rr):
    if rr in SPECIAL_ROWS:
        return (-8 if rr % 2 == 0 else -9), 16
    return (-4, 8) if rr % 2 == 0 else (-5, 10)


CHUNKS = [[(-4, 16), (12, 8)], [(-3, 16), (13, 6)], [(-2, 16), (14, 4)], [(0, 16)]]


class Buf:
    __slots__ = ("lw", "rd")

    def __init__(self):
        self.lw = None
        self.rd = []


class FW:
    ENGS = ("pe", "dve", "act", "pool", "sp")

    def __init__(self, nc):
        self.nc = nc
        self.ins = {e: [] for e in self.ENGS}
        self.dma_cnt = {}
        self.dma_keys = []

    def _deps(self, reads, writes):
        deps = []
        for b in reads:
            if b.lw is not None:
                deps.append(b.lw)
        for b in writes:
            if b.lw is not None:
                deps.append(b.lw)
            deps.extend(b.rd)
        return deps

    def op(self, eng, fn, reads=(), writes=()):
        deps = self._deps(reads, writes)
        idx = len(self.ins[eng])
        self.ins[eng].append([fn, deps, None])
        tok = ("e", eng, idx)
        for b in writes:
            b.lw = tok
            b.rd = []
        for b in reads:
            b.rd.append(tok)
        return tok

    def dma(self, eng, key, fn, reads=(), writes=()):
        deps = self._deps(reads, writes)
        if key not in self.dma_cnt:
            self.dma_cnt[key] = 0
            self.dma_keys.append(key)
        self.dma_cnt[key] += 16
        tok = ("d", key, self.dma_cnt[key])
        self.ins[eng].append([fn, deps, key])
        for b in writes:
            b.lw = tok
            b.rd = []
        for b in reads:
            b.rd.append(tok)
        return tok

    def barrier(self):
        toks = []
        for e in self.ENGS:
            for idx in range(len(self.ins[e]) - 1, -1, -1):
                fn, deps, key = self.ins[e][idx]
                if fn is not None and key is None:
                    toks.append(("e", e, idx))
                    break
        for k in self.dma_keys:
            toks.append(("d", k, self.dma_cnt[k]))
        for e in self.ENGS:
            self.ins[e].append([None, list(toks), None])

    def finalize(self):
        nc = self.nc
        needed = {e: set() for e in self.ENGS}
        for e in self.ENGS:
            for idx, (fn, deps, key) in enumerate(self.ins[e]):
                for d in deps:
                    if d[0] == "e" and (d[1] != e or key is not None or e != "pe"):
                        needed[d[1]].add(d[2])
        rank = {}
        for e in self.ENGS:
            r = 0
            for idx in range(len(self.ins[e])):
                if idx in needed[e]:
                    r += 1
                    rank[(e, idx)] = r
        es = contextlib.ExitStack()
        esem = {e: es.enter_context(nc.semaphore("s_" + e)) for e in self.ENGS}
        dsem = {k: es.enter_context(nc.semaphore("d_%d" % i)) for i, k in enumerate(self.dma_keys)}
        block = es.enter_context(nc.Block())
        ins = self.ins

        def make(e):
            def body(engine):
                waited = {}
                for idx, (fn, deps, key) in enumerate(ins[e]):
                    for d in deps:
                        if d[0] == "e":
                            if d[1] == e and key is None and e == "pe":
                                continue
                            wk = ("e", d[1])
                            val = rank[(d[1], d[2])]
                            sem = esem[d[1]]
                        else:
                            wk = ("d", d[1])
                            val = d[2]
                            sem = dsem[d[1]]
                        if waited.get(wk, 0) >= val:
                            continue
                        waited[wk] = val
                        engine.wait_ge(sem, val)
                    if fn is None:
                        continue
                    inst = fn(engine)
                    if key is not None:
                        inst.then_inc(dsem[key], 16)
                    elif idx in needed[e]:
                        inst.then_inc(esem[e], 1)
            return body

        block.tensor(make("pe"))
        block.vector(make("dve"))
        block.scalar(make("act"))
        block.gpsimd(make("pool"))
        block.sync(make("sp"))
        es.close()


def build_fused():
    nc = bass.Bass("TRN2", target_bir_lowering=False)
    fw = FW(nc)

    def din(name, shape, dt=F32):
        return nc.dram_tensor(name, shape, dt, kind="ExternalInput").ap()

    hx = din("hx", [XT * 128, 1024])
    valid = din("valid", [128, XT])
    g_fin = din("g_fin", [128, 1024])
    L = []
    for i in range(4):
        d = {"g_mix": din("g_mix%d" % i, [128, 1024]), "g_ffn": din("g_ffn%d" % i, [128, 1024]),
             "wq": din("wq%d" % i, [1024, 2048]), "keysT": din("keysT%d" % i, [128, 2048]),
             "wupT": din("wupT%d" % i, [1024, 16384]), "wdown": din("wdown%d" % i, [16384, 1024])}
        if i % 2 == 1:
            d["w_in"] = din("w_in%d" % i, [1024, 3072])
            d["w_cv"] = din("w_cv%d" % i, [128, 24])
            d["w_out"] = din("w_out%d" % i, [1024, 1024])
        else:
            d["w_qkv"] = din("w_qkv%d" % i, [1024, 3072])
            d["w_o"] = din("w_o%d" % i, [1024, 1024])
            d["btab"] = din("btab%d" % i, [TROWS, 8, 64, 2048])
        L.append(d)
    hout = nc.dram_tensor("hout", [2048, 1024], F32, kind="ExternalOutput").ap()
    gscr = nc.dram_tensor("gscr", [NT, 2, 128, 8192], BF16).ap()
    HD = [nc.dram_tensor("hd%d" % i, [XT * 128, 1024], F32).ap() for i in range(2)]

    top = contextlib.ExitStack()

    _cnt = [0]

    def sb(es, name, shape, dt):
        _cnt[0] += 1
        return es.enter_context(nc.sbuf_tensor("%s_%d" % (name, _cnt[0]), shape, dt))

    P = [top.enter_context(nc.psum_tensor("ps%d" % i, [128, 512], F32)) for i in range(8)]
    PB = [Buf() for _ in range(8)]

    h_sb = sb(top, "h_sb", [128, NT, 1024], F32)
    HB = [Buf() for _ in range(NT)]
    ident = sb(top, "ident", [128, 128], F32)
    identb = sb(top, "identb", [128, 128], BF16)
    iot = sb(top, "iot", [128, 128], F32)
    iotb = sb(top, "iotb", [128, 128], BF16)
    pidx = sb(top, "pidx", [128, 1], F32)
    CONST = Buf()

    def pe_mm(out, lhsT, rhs, start, stop, reads, writes):
        fw.op("pe", lambda e: e.matmul(out, lhsT, rhs, start=start, stop=stop), reads, writes)

    def pe_tr(out, in_, idn, reads, writes):
        fw.op("pe", lambda e: e.transpose(out=out, in_=in_, identity=idn), reads, writes)

    def tt(eng, out, in0, in1, op, reads, writes):
        fw.op(eng, lambda e: e.tensor_tensor(out=out, in0=in0, in1=in1, op=op), reads, writes)

    def ts(eng, out, in0, s1, s2, op0, op1, reads, writes):
        if op1 is None:
            fw.op(eng, lambda e: e.tensor_scalar(out=out, in0=in0, scalar1=s1, scalar2=None, op0=op0), reads, writes)
        else:
            fw.op(eng, lambda e: e.tensor_scalar(out=out, in0=in0, scalar1=s1, scalar2=s2, op0=op0, op1=op1), reads, writes)

    def stt(eng, out, in0, scalar, in1, op0, op1, reads, writes):
        fw.op(eng, lambda e: e.scalar_tensor_tensor(out=out, in0=in0, scalar=scalar, in1=in1, op0=op0, op1=op1), reads, writes)

    def cp(eng, out, in_, reads, writes):
        if eng == "act":
            fw.op(eng, lambda e: e.copy(out=out, in_=in_), reads, writes)
        else:
            fw.op(eng, lambda e: e.tensor_copy(out=out, in_=in_), reads, writes)

    def act(out, in_, func, reads, writes, bias=None, scale=None, accum_out=None):
        kw = {}
        if bias is not None:
            kw["bias"] = bias
        if scale is not None:
            kw["scale"] = scale
        if accum_out is not None:
            kw["accum_out"] = accum_out
        fw.op("act", lambda e: e.activation(out=out, in_=in_, func=func, **kw), reads, writes)

    def dma(eng, key, out, in_, reads, writes):
        fw.dma(eng, key, lambda e: e.dma_start(out=out, in_=in_), reads, writes)

    fw.op("pool", lambda e: e.iota(iot[:], pattern=[[1, 128]], base=0, channel_multiplier=0,
                                   allow_small_or_imprecise_dtypes=True), writes=[CONST])
    fw.op("pool", lambda e: e.iota(pidx[:], pattern=[[0, 1]], base=0, channel_multiplier=1,
                                   allow_small_or_imprecise_dtypes=True), writes=[CONST])
    ts("dve", ident[:], iot[:], pidx[:, 0:1], None, ALU.is_equal, None, [CONST], [CONST])
    cp("dve", identb[:], ident[:], [CONST], [CONST])
    cp("dve", iotb[:], iot[:], [CONST], [CONST])

    def rmsnorm(es_bufs, src_ap, src_bufs, gain_t, gain_buf, out_ap, out_buf):
        junk, JB, st, SB_ = es_bufs
        act(junk[:], src_ap, AF.Square, src_bufs, [JB, SB_], accum_out=st[:, 0:1])
        ts("dve", st[:, 1:2], st[:, 0:1], 1.0 / 1024.0, 1e-6, ALU.mult, ALU.add, [SB_], [SB_])
        act(st[:, 2:3], st[:, 1:2], AF.Sqrt, [SB_], [SB_])
        fw.op("dve", lambda e: e.reciprocal(out=st[:, 3:4], in_=st[:, 2:3]), [SB_], [SB_])
        stt("dve", out_ap, src_ap, st[:, 3:4], gain_t[:], ALU.mult, ALU.mult, src_bufs + [SB_, gain_buf], [out_buf])

    def to_featmajor(src, src_buf, dstT, col0, dst_buf, banks=(0, 1)):
        for half in range(2):
            b = banks[half]
            for kk in range(4):
                k = half * 4 + kk
                pe_tr(P[b][:, kk * 128:(kk + 1) * 128], src[:, k * 128:(k + 1) * 128], ident[:],
                      [src_buf, CONST], [PB[b]])
            cp("act", dstT[:, half * 4:(half + 1) * 4, col0:col0 + 128],
               P[b][:].rearrange("p (k t) -> p k t", k=4), [PB[b]], [dst_buf])

    def load_w_bf(w_ap, col0, ncols, dst, dst_col0, dst_buf, stage, stage_buf, key, piece=256):
        wv = w_ap.rearrange("(k p) n -> p k n", p=128)
        for c in range(0, ncols, piece):
            dma("sp", key, stage[:, :, 0:piece], wv[:, :, col0 + c: col0 + c + piece], [], [stage_buf])
            cp("pool", dst[:, :, dst_col0 + c: dst_col0 + c + piece], stage[:, :, 0:piece], [stage_buf], [dst_buf])

    def emit_chunk(li, kind, src, dst, o0, n, final):
        W = L[li]
        g_mix, g_ffn = W['g_mix'], W['g_ffn']
        wq, keysT, wupT, wdown = W['wq'], W['keysT'], W['wupT'], W['wdown']
        ET = (n + 8) * 128
        HB = [Buf() for _ in range(NT)]
        for t in range(n):
            dma('sp', 'h%d' % (t % 4), h_sb[:, t, :], src[(o0 + XOFF + t) * 128:(o0 + XOFF + t + 1) * 128, :], [], [HB[t]])
        if kind == "conv":
            mx = contextlib.ExitStack()
            hnT = sb(mx, "c_hnT", [128, 8, 2304], BF16)
            HNT = Buf()
            gain = sb(mx, "c_gain", [128, 1024], F32)
            GB = Buf()
            junk = sb(mx, "c_junk", [128, 1024], BF16)
            st = sb(mx, "c_st", [128, 4], F32)
            hn = sb(mx, "c_hn", [128, 1024], F32)
            xt = sb(mx, "c_xt", [128, 1024], F32)
            JB, SB_, HN, XTB = Buf(), Buf(), Buf(), Buf()
            dma("sp", "gain", gain[:], g_mix, [], [GB])
            w_in, w_cv, w_out = W["w_in"], W["w_cv"], W["w_out"]
            NU = (n + 2) * 128
            for i in range(n + 2):
                if 1 <= i < n + 1:
                    t = i - 1
                    tsrc, sbufs = h_sb[:, t, :], [HB[t]]
                else:
                    at = o0 + XOFF - 1 + i
                    dma("sp", "cx", xt[:], src[at * 128:(at + 1) * 128, :], [], [XTB])
                    tsrc, sbufs = xt[:], [XTB]
                rmsnorm((junk, JB, st, SB_), tsrc, sbufs, gain, GB, hn[:], HN)
                to_featmajor(hn, HN, hnT, i * 128, HNT)
            stage = sb(mx, "c_stage", [128, 8, 128], F32)
            STG = Buf()
            wcv = sb(mx, "c_wcv", [128, 24], F32)
            WCV = Buf()
            dma("sp", "wcv", wcv[:], w_cv, [], [WCV])
            zT = sb(mx, "c_zT", [128, 8, 2048], BF16)
            ZT = Buf()
            wtri = sb(mx, "c_wtri", [128, 8, 384], BF16)
            WTRI = Buf()
            u = sb(mx, "c_u", [128, 2304], F32)
            gbt = sb(mx, "c_gb", [128, 2304], F32)
            gct = sb(mx, "c_gc", [128, 384], F32)
            t1 = sb(mx, "c_t1", [128, 2048], F32)
            U, GBT, GCT, T1 = Buf(), Buf(), Buf(), Buf()
            for c in range(8):
                for j in range(3):
                    load_w_bf(w_in, j * 1024 + c * 128, 128, wtri, j * 128, WTRI, stage, STG, "wst", piece=128)
                for tc_ in range((n + 2) // 2):
                    c0 = tc_ * 256
                    for j in range(3):
                        b = (tc_ % 2) * 3 + j
                        for k in range(8):
                            pe_mm(P[b][:, 0:256], wtri[:, k, j * 128:(j + 1) * 128], hnT[:, k, c0:c0 + 256],
                                  k == 0, k == 7, [WTRI, HNT], [PB[b]])
                    b0 = (tc_ % 2) * 3
                    cp("act", gbt[:, c0:c0 + 256], P[b0][:, 0:256], [PB[b0]], [GBT])
                    cp("act", gct[:, 0:256], P[b0 + 1][:, 0:256], [PB[b0 + 1]], [GCT])
                    tt("dve", u[:, c0:c0 + 256], gct[:, 0:256], P[b0 + 2][:, 0:256], ALU.mult, [GCT, PB[b0 + 2]], [U])
                NO = n * 128
                ts("dve", t1[:, 0:NO], u[:, 128:128 + NO], wcv[:, 8 + c:9 + c], None, ALU.mult, None, [U, WCV], [T1])
                stt("dve", t1[:, 0:NO], u[:, 127:127 + NO], wcv[:, c:c + 1], t1[:, 0:NO], ALU.mult, ALU.add, [U, WCV, T1], [T1])
                stt("dve", t1[:, 0:NO], u[:, 129:129 + NO], wcv[:, 16 + c:17 + c], t1[:, 0:NO], ALU.mult, ALU.add, [U, WCV, T1], [T1])
                tt("dve", zT[:, c, 0:NO], t1[:, 0:NO], gbt[:, 128:128 + NO], ALU.mult, [T1, GBT], [ZT])
            wo_bf = sb(mx, "c_wo", [128, 8, 1024], BF16)
            WO = Buf()
            load_w_bf(w_out, 0, 1024, wo_bf, 0, WO, stage, STG, "wst", piece=128)
            for t in range(n):
                for half in range(2):
                    b = 6 + half
                    for k in range(8):
                        pe_mm(P[b][:], zT[:, k, t * 128:(t + 1) * 128], wo_bf[:, k, half * 512:(half + 1) * 512],
                              k == 0, k == 7, [ZT, WO], [PB[b]])
                    tt("dve", h_sb[:, t, half * 512:(half + 1) * 512], h_sb[:, t, half * 512:(half + 1) * 512],
                       P[b][:], ALU.add, [HB[t], PB[b]], [HB[t]])
            fw.barrier()
            mx.close()

        if kind == "attn":
            mx = contextlib.ExitStack()
            m1 = contextlib.ExitStack()
            aoT = sb(mx, "a_aoT", [128, 8, 2048], BF16)
            AOT = Buf()
            hnT = sb(m1, "a_hnT", [128, 8, EXT_TOK], BF16)
            HNT = Buf()
            m0 = contextlib.ExitStack()
            gain = sb(m0, "a_gain", [128, 1024], F32)
            GB = Buf()
            junk = sb(m0, "a_junk", [128, 1024], BF16)
            st = sb(m0, "a_st", [128, 4], F32)
            hn = sb(m0, "a_hn", [128, 1024], F32)
            xt = sb(m0, "a_xt", [128, 1024], F32)
            JB, SB_, HN, XTB = Buf(), Buf(), Buf(), Buf()
            dma("sp", "gain", gain[:], g_mix, [], [GB])
            w_qkv, w_o, btab = W["w_qkv"], W["w_o"], W["btab"]
            for i in range(n + 8):
                if 4 <= i < n + 4:
                    t = i - 4
                    tsrc, sbufs = h_sb[:, t, :], [HB[t]]
                else:
                    at = o0 + XOFF - 4 + i
                    dma("sp", "cx", xt[:], src[at * 128:(at + 1) * 128, :], [], [XTB])
                    tsrc, sbufs = xt[:], [XTB]
                rmsnorm((junk, JB, st, SB_), tsrc, sbufs, gain, GB, hn[:], HN)
                to_featmajor(hn, HN, hnT, i * 128, HNT)
            fw.barrier()
            m0.close()
            stage = sb(m1, "a_stage", [128, 8, 128], F32)
            STG = Buf()
            wtri = sb(m1, "a_wtri", [128, 8, 384], BF16)
            WTRI = Buf()
            QT = sb(m1, "a_QT", [128, 2048], BF16)
            KT = sb(m1, "a_KT", [128, EXT_TOK], BF16)
            V = sb(m1, "a_V", [128, EXT_TOK // 128, 128], BF16)
            QTB, KTB, VB = Buf(), Buf(), Buf()
            tab2 = [sb(m1, "a_tab%d" % i, [64, 2048], F32) for i in range(2)]
            TAB2 = [[Buf()], [Buf()]]
            s_sb2 = [sb(m1, "a_s%d" % i, [64, 1024], F32) for i in range(2)]
            p_bf2 = [sb(m1, "a_p%d" % i, [64, 1024], BF16) for i in range(2)]
            pT2 = [sb(m1, "a_pT%d" % i, [128, 8, 64], BF16) for i in range(2)]
            sm4 = [sb(m1, "a_sm%d" % i, [64, 8], F32) for i in range(4)]
            o_sb2 = [sb(m1, "a_o%d" % i, [64, 128], F32) for i in range(2)]
            S2, PBF2, PT2 = [Buf(), Buf()], [Buf(), Buf()], [Buf(), Buf()]
            SM4 = [Buf() for _ in range(4)]
            OSB2 = [Buf(), Buf()]
            for hp in range(8):
                for j in range(3):
                    load_w_bf(w_qkv, j * 1024 + hp * 128, 128, wtri, j * 128, WTRI, stage, STG, "wst", piece=128)
                for c4 in range(n // 4):
                    b = c4 % 2
                    for k in range(8):
                        pe_mm(P[b][:], wtri[:, k, 0:128], hnT[:, k, OWN0 + c4 * 512: OWN0 + (c4 + 1) * 512],
                              k == 0, k == 7, [WTRI, HNT], [PB[b]])
                    cp("act", QT[:, c4 * 512:(c4 + 1) * 512], P[b][:], [PB[b]], [QTB])
                for c6 in range((n + 8) // 4):
                    b = c6 % 2
                    for k in range(8):
                        pe_mm(P[b][:], wtri[:, k, 128:256], hnT[:, k, c6 * 512:(c6 + 1) * 512],
                              k == 0, k == 7, [WTRI, HNT], [PB[b]])
                    cp("act", KT[:, c6 * 512:(c6 + 1) * 512], P[b][:], [PB[b]], [KTB])
                for vt in range(n + 8):
                    b = 2 + (vt // 4) % 2
                    q4 = vt % 4
                    for k in range(8):
                        pe_mm(P[b][:, q4 * 128:(q4 + 1) * 128], hnT[:, k, vt * 128:(vt + 1) * 128], wtri[:, k, 256:384],
                              k == 0, k == 7, [WTRI, HNT], [PB[b]])
                    if q4 == 3:
                        cp("act", V[:, vt - 3:vt + 1, :], P[b][:].rearrange("p (a n) -> p a n", a=4), [PB[b]], [VB])
                def st_a(k):
                    lr, hh = k // 2, k % 2
                    e_ = HALO + lr
                    dk, nk = _rowwin(2 * o0 + lr)
                    kb = e_ + dk
                    NKC = nk * 64
                    tab, TAB = tab2[lr % 2], TAB2[lr % 2]
                    if hh == 0:
                        dma("sp", "tab%d" % (lr % 2), tab[:], btab[2 * o0 + lr + 8, hp], [], TAB)
                    pl = hh * 64
                    s_sb, p_bf, sm = s_sb2[hh], p_bf2[hh], sm4[k % 4]
                    S, PBF, SM = S2[hh], PBF2[hh], SM4[k % 4]
                    sbanks = (4, 5) if hh == 0 else (0, 1)
                    for half in range(2):
                        c0, c1 = half * 512, min((half + 1) * 512, NKC)
                        if c1 <= c0:
                            continue
                        b = sbanks[half]
                        pe_mm(P[b][0:64, 0:c1 - c0], QT[pl:pl + 64, lr * 64:(lr + 1) * 64],
                              KT[pl:pl + 64, kb * 64 + c0: kb * 64 + c1],
                              True, True, [QTB, KTB], [PB[b]])
                        stt("dve", s_sb[:, c0:c1], P[b][0:64, 0:c1 - c0], 0.125,
                            tab[:, hh * 1024 + c0: hh * 1024 + c1], ALU.mult, ALU.add,
                            [PB[b]] + TAB, [S])
                    fw.op("dve", (lambda sm, s_sb, NKC: lambda e: e.reduce_max(out=sm[:, 0:1], in_=s_sb[:, 0:NKC], axis=AX.X))(sm, s_sb, NKC), [S], [SM])
                    ts("dve", sm[:, 1:2], sm[:, 0:1], -1.0, None, ALU.mult, None, [SM], [SM])
                    act(p_bf[:, 0:NKC], s_sb[:, 0:NKC], AF.Exp, [S, SM], [PBF, SM], bias=sm[:, 1:2], accum_out=sm[:, 2:3])

                def st_b(k):
                    lr, hh = k // 2, k % 2
                    dk, nk = _rowwin(2 * o0 + lr)
                    NC = nk // 2
                    p_bf, pT = p_bf2[hh], pT2[hh]
                    PBF, PT = PBF2[hh], PT2[hh]
                    tbank = 6 if hh == 0 else 2
                    pTp = P[tbank][:].bitcast(BF16)
                    for c in range(NC):
                        pe_tr(pTp[:, c * 64:(c + 1) * 64], p_bf[:, c * 128:(c + 1) * 128], identb[0:64, 0:64],
                              [PBF, CONST], [PB[tbank]])
                    cp("act", pT[:, 0:NC, :], pTp[:, 0:NC * 64].rearrange("p (c q) -> p c q", c=NC), [PB[tbank]], [PT])

                def st_c(k):
                    lr, hh = k // 2, k % 2
                    e_ = HALO + lr
                    dk, nk = _rowwin(2 * o0 + lr)
                    kb = e_ + dk
                    NC = nk // 2
                    pT, sm = pT2[hh], sm4[k % 4]
                    PT, SM = PT2[hh], SM4[k % 4]
                    o_sb, OSB = o_sb2[lr % 2], OSB2[lr % 2]
                    vbank = 7 if hh == 0 else 3
                    for c in range(NC):
                        pe_mm(P[vbank][0:64, hh * 64:(hh + 1) * 64], pT[:, c, :], V[:, kb // 2 + c, hh * 64:(hh + 1) * 64],
                              c == 0, c == NC - 1, [PT, VB], [PB[vbank]])
                    fw.op("dve", (lambda sm: lambda e: e.reciprocal(out=sm[:, 3:4], in_=sm[:, 2:3]))(sm), [SM], [SM])
                    ts("dve", o_sb[:, hh * 64:(hh + 1) * 64], P[vbank][0:64, hh * 64:(hh + 1) * 64], sm[:, 3:4], None,
                       ALU.mult, None, [PB[vbank], SM], [OSB])
                    if hh == 1:
                        pe_tr(P[3][:, 0:64], o_sb[:], ident[0:64, 0:64], [OSB, CONST], [PB[3]])
                        cp("act", aoT[:, hp, lr * 64:(lr + 1) * 64], P[3][:, 0:64], [PB[3]], [AOT])

                NK = 4 * n
                for k in range(NK + 2):
                    if k < NK:
                        st_a(k)
                    if 1 <= k < NK + 1:
                        st_b(k - 1)
                    if k >= 2:
                        st_c(k - 2)
            fw.barrier()
            m1.close()
            stage2 = sb(mx, "a_stage2", [128, 8, 256], F32)
            STG2 = Buf()
            wo_bf = sb(mx, "a_wo", [128, 8, 1024], BF16)
            WO = Buf()
            load_w_bf(w_o, 0, 1024, wo_bf, 0, WO, stage2, STG2, "wst2")
            for t in range(n):
                for half in range(2):
                    b = half
                    for k in range(8):
                        pe_mm(P[b][:], aoT[:, k, t * 128:(t + 1) * 128], wo_bf[:, k, half * 512:(half + 1) * 512],
                              k == 0, k == 7, [AOT, WO], [PB[b]])
                    tt("dve", h_sb[:, t, half * 512:(half + 1) * 512], h_sb[:, t, half * 512:(half + 1) * 512],
                       P[b][:], ALU.add, [HB[t], PB[b]], [HB[t]])
            fw.barrier()
            mx.close()

        pz = contextlib.ExitStack()
        hnT = sb(pz, "p_hnT", [128, 8, 2048], BF16)
        HNT = [Buf() for _ in range(NT)]
        p0 = contextlib.ExitStack()
        gain = sb(p0, "p_gain", [128, 1024], F32)
        GB = Buf()
        junk = sb(p0, "p_junk", [128, 1024], BF16)
        st = sb(p0, "p_st", [128, 4], F32)
        hn = sb(p0, "p_hn", [128, 1024], F32)
        JB, SB_, HN = Buf(), Buf(), Buf()
        dma("sp", "gain", gain[:], g_ffn, [], [GB])
        for t in range(n):
            rmsnorm((junk, JB, st, SB_), h_sb[:, t, :], [HB[t]], gain, GB, hn[:], HN)
            to_featmajor(hn, HN, hnT, t * 128, HNT[t])
        fw.barrier()
        p0.close()

        p1 = contextlib.ExitStack()
        wq_bf = sb(p1, "p_wq", [128, 8, 2048], BF16)
        WQ = Buf()
        kT_bf = sb(p1, "p_kT", [128, 16, 128], BF16)
        KTB = Buf()
        ps_ = contextlib.ExitStack()
        stage = sb(ps_, "p_stage", [128, 8, 256], F32)
        STG = Buf()
        load_w_bf(wq, 0, 2048, wq_bf, 0, WQ, stage, STG, "wst")
        dma("sp", "wst", stage[:].rearrange("p k n -> p (k n)"), keysT, [], [STG])
        cp("pool", kT_bf[:].rearrange("p c n -> p (c n)"), stage[:].rearrange("p k n -> p (k n)"), [STG], [KTB])
        fw.barrier()
        ps_.close()
        qT_sb = sb(p1, "p_qT", [128, 16, 128], BF16)
        sc_sb2 = [sb(p1, "p_sc%d" % i, [128, 16, 128], F32) for i in range(2)]
        sc2 = sb(p1, "p_sc2", [128, 128], F32)
        sv = sb(p1, "p_sv", [128, 16, 16], F32)
        si = sb(p1, "p_si", [128, 16, 16], U32)
        cand = sb(p1, "p_cand", [128, 8, 256], F32)
        cand2 = sb(p1, "p_cand2", [128, 256], F32)
        best = sb(p1, "p_best", [128, 8, 16], F32)
        pos = sb(p1, "p_pos", [128, 8, 16], U32)
        gex = sb(p1, "p_gex", [128, 8, 16], F32)
        gz = sb(p1, "p_gz", [128, 16], F32)
        gate = sb(p1, "p_gate", [128, 8, 16], F32)
        au = sb(p1, "p_au", [128, 8, 16], U32)
        bu = sb(p1, "p_bu", [128, 8, 16], U32)
        af = sb(p1, "p_af", [128, 8, 16], F32)
        bf = sb(p1, "p_bf", [128, 8, 16], F32)
        sif = sb(p1, "p_sif", [128, 16, 16], F32)
        i_f = sb(p1, "p_if", [128, 128], F32)
        j_f = sb(p1, "p_jf", [128, 128], F32)
        iT = sb(p1, "p_iT", [128, 128], F32)
        jT = sb(p1, "p_jT", [128, 128], F32)
        gT = sb(p1, "p_gT", [128, 128], F32)
        A_s2 = [sb(p1, "p_A%d" % i, [128, 16, 128], BF16) for i in range(2)]
        B_s2 = [sb(p1, "p_B%d" % i, [128, 16, 128], BF16) for i in range(2)]
        G_sb = sb(p1, "p_G", [128, 128, 64], BF16)
        QTB = [Buf() for _ in range(4)]
        SCB2 = [[Buf() for _ in range(4)] for _ in range(2)]
        R = Buf()
        TR = Buf()
        AB2 = [[Buf() for _ in range(16)] for _ in range(2)]
        BB2 = [[Buf() for _ in range(16)] for _ in range(2)]
        GSB = Buf()
        GSCR = [[Buf(), Buf()] for _ in range(NT)]
        svv = sv[:].rearrange("p (h two) k -> p h two k", two=2)
        siv = sif[:].rearrange("p (h two) k -> p h two k", two=2)
        iot16 = iot[:, 0:16].unsqueeze(1).unsqueeze(1).to_broadcast([128, 8, 16, 16])
        iot_b = iotb[:, :].unsqueeze(1).to_broadcast([128, 16, 128])
        cand4 = cand[:].rearrange("p h (a b) -> p h a b", a=16)
        def top16(vals_ap, scratch_ap, out_v, out_i, rbufs):
            fw.op("dve", lambda e: e.max(out=out_v[:, 0:8], in_=vals_ap), rbufs + [R], [R])
            fw.op("dve", lambda e: e.max_index(out=out_i[:, 0:8], in_max=out_v[:, 0:8], in_values=vals_ap), rbufs + [R], [R])
            fw.op("dve", lambda e: e.match_replace(out=scratch_ap, in_to_replace=out_v[:, 0:8], in_values=vals_ap,
                                                   imm_value=NEG), rbufs + [R], [R])
            fw.op("dve", lambda e: e.max(out=out_v[:, 8:16], in_=scratch_ap), [R], [R])
            fw.op("dve", lambda e: e.max_index(out=out_i[:, 8:16], in_max=out_v[:, 8:16], in_values=scratch_ap), [R], [R])

        def p1_front(t):
            sc_sb, SCB = sc_sb2[t % 2], SCB2[t % 2]
            for cb in range(4):
                for cc in range(4):
                    c = cb * 4 + cc
                    for k in range(8):
                        pe_mm(P[cb][:, cc * 128:(cc + 1) * 128], wq_bf[:, k, c * 128:(c + 1) * 128],
                              hnT[:, k, t * 128:(t + 1) * 128], k == 0, k == 7, [WQ, HNT[t]], [PB[cb]])
                cp("act", qT_sb[:, cb * 4:(cb + 1) * 4, :], P[cb][:].rearrange("p (c n) -> p c n", c=4), [PB[cb]], [QTB[cb]])
            for cb in range(4):
                for cc in range(4):
                    c = cb * 4 + cc
                    pe_mm(P[4 + cb][:, cc * 128:(cc + 1) * 128], qT_sb[:, c, :], kT_bf[:, c, :], True, True,
                          [QTB[cb], KTB], [PB[4 + cb]])
                cp("act", sc_sb[:, cb * 4:(cb + 1) * 4, :], P[4 + cb][:].rearrange("p (c n) -> p c n", c=4),
                   [PB[4 + cb]], [SCB[cb]])


        def p1_mid(t):
            sc_sb, SCB = sc_sb2[t % 2], SCB2[t % 2]
            for c in range(16):
                top16(sc_sb[:, c, :], sc2[:], sv[:, c, :], si[:, c, :], [SCB[c // 4]])
            tt("dve", cand4, svv[:, :, 0, :].unsqueeze(3).to_broadcast([128, 8, 16, 16]),
               svv[:, :, 1, :].unsqueeze(2).to_broadcast([128, 8, 16, 16]), ALU.add, [R], [R])
            for h in range(8):
                top16(cand[:, h, :], cand2[:], best[:, h, :], pos[:, h, :], [])
            tt("dve", gex[:], best[:], best[:, :, 0:1].to_broadcast([128, 8, 16]), ALU.subtract, [R], [R])
            act(gex[:], gex[:], AF.Exp, [R], [R])
            fw.op("dve", lambda e: e.reduce_sum(out=gz[:, 0:8], in_=gex[:], axis=AX.X), [R], [R])
            fw.op("dve", lambda e: e.reciprocal(out=gz[:, 8:16], in_=gz[:, 0:8]), [R], [R])
            tt("dve", gate[:], gex[:], gz[:, 8:16].unsqueeze(2).to_broadcast([128, 8, 16]), ALU.mult, [R], [R])
            fw.op("dve", lambda e: e.tensor_single_scalar(out=au[:], in_=pos[:], scalar=4, op=ALU.logical_shift_right), [R], [R])
            fw.op("dve", lambda e: e.tensor_single_scalar(out=bu[:], in_=pos[:], scalar=15, op=ALU.bitwise_and), [R], [R])
            cp("dve", af[:], au[:], [R], [R])
            cp("dve", bf[:], bu[:], [R], [R])
            cp("dve", sif[:], si[:], [R], [R])
            for (xf, half, dsti) in ((af, 0, i_f), (bf, 1, j_f)):
                tt("dve", cand4, xf[:].unsqueeze(3).to_broadcast([128, 8, 16, 16]), iot16, ALU.is_equal, [R, CONST], [R])
                tt("dve", cand4, cand4, siv[:, :, half, :].unsqueeze(2).to_broadcast([128, 8, 16, 16]), ALU.mult, [R], [R])
                fw.op("dve", (lambda d_: lambda e: e.tensor_reduce(
                    out=d_[:], in_=cand[:].rearrange("p h (k a) -> p (h k) a", a=16), axis=AX.X, op=ALU.add))(dsti), [R], [R])

        def p1_tr(t):
            for (srci, dstT, b) in ((i_f[:], iT, 0), (j_f[:], jT, 1), (gate[:].rearrange("p h k -> p (h k)"), gT, 2)):
                pe_tr(P[b][:, 0:128], srci, ident[:], [R, CONST], [PB[b]])
                cp("act", dstT[:], P[b][:, 0:128], [PB[b]], [TR])

        def p1_back(t):
            for hb in range(2):
                for sbk in range(4):
                    t0 = hb * 64 + sbk * 16
                    A_s, B_s, AB, BB = A_s2[sbk % 2], B_s2[sbk % 2], AB2[sbk % 2], BB2[sbk % 2]
                    for tl in range(16):
                        tok = t0 + tl
                        ts("dve", A_s[:, tl, :], iotb[:], iT[:, tok:tok + 1], gT[:, tok:tok + 1], ALU.is_equal, ALU.mult,
                           [TR, CONST], [AB[tl]])
                        ts("dve", B_s[:, tl, :], iotb[:], jT[:, tok:tok + 1], None, ALU.is_equal, None,
                           [TR, CONST], [BB[tl]])
                    for q in range(4):
                        b = q
                        for x in range(4):
                            tl = q * 4 + x
                            pe_mm(P[b][:, x * 128:(x + 1) * 128], B_s[:, tl, :], A_s[:, tl, :], True, True,
                                  [AB[tl], BB[tl]], [PB[b]])
                        tk = sbk * 16 + q * 4
                        cp("act", G_sb[:, :, tk:tk + 4].rearrange("p n t -> p t n"),
                           P[b][:].rearrange("p (t n) -> p t n", t=4), [PB[b]], [GSB])
                dma("pool", "gst", gscr[t, hb], G_sb[:].rearrange("p n t -> p (n t)"), [GSB], [GSCR[t][hb]])

        p1_front(0)
        for t in range(n):
            p1_mid(t)
            if t + 1 < n:
                p1_front(t + 1)
            p1_tr(t)
            p1_back(t)
        fw.barrier()
        p1.close()

        p2 = contextlib.ExitStack()
        stU = [sb(p2, "e_stU%d" % i, [128, 8, 256], F32) for i in range(2)]
        stD = [sb(p2, "e_stD%d" % i, [128, 2, 1024], F32) for i in range(2)]
        wu = [sb(p2, "e_wu%d" % i, [128, 8, 512], BF16) for i in range(2)]
        wd = [sb(p2, "e_wd%d" % i, [128, 4, 1024], BF16) for i in range(2)]
        gsl = [sb(p2, "e_gs%d" % i, [128, 2, 4, 64], BF16) for i in range(3)]
        ge = [sb(p2, "e_ge%d" % i, [128, 4, 128], F32) for i in range(3)]
        ab = [sb(p2, "e_ab%d" % i, [128, 4, 128], BF16) for i in range(3)]
        STU, STD = [Buf(), Buf()], [Buf(), Buf()]
        WU, WD = [Buf(), Buf()], [Buf(), Buf()]
        GSL, GE, ABB = [Buf() for _ in range(3)], [Buf() for _ in range(3)], [Buf() for _ in range(3)]
        wupv = wupT.rearrange("(k p) e -> p k e", p=128)
        wdnv = wdown.rearrange("(n q) d -> q n d", q=128)
        NG = 32
        its = [(g, t) for g in range(NG) for t in range(n)]

        def load_group_dma(g):
            for hf in range(2):
                dma("sp", "wu%d" % hf, stU[hf][:], wupv[:, :, g * 512 + hf * 256: g * 512 + (hf + 1) * 256], [], [STU[hf]])
                dma("sp", "wd%d" % hf, stD[hf][:], wdnv[:, g * 4 + hf * 2: g * 4 + (hf + 1) * 2, :], [], [STD[hf]])

        def load_group_cast(g):
            wpar = g % 2
            for hf in range(2):
                cp("act", wu[wpar][:, :, hf * 256:(hf + 1) * 256], stU[hf][:], [STU[hf]], [WU[wpar]])
            cp("act", wd[wpar][:, 0:2, :], stD[0][:], [STD[0]], [WD[wpar]])
            cp("pool", wd[wpar][:, 2:4, :], stD[1][:], [STD[1]], [WD[wpar]])

        load_group_dma(0)
        load_group_cast(0)

        def stage_up(k):
            g, t = its[k]
            r3, wpar = k % 3, g % 2
            if t == 0 and g + 1 < NG:
                load_group_dma(g + 1)
            for hb in range(2):
                dma("sp", "gsl%d" % r3, gsl[r3][:, hb].rearrange("p n t -> p (n t)"),
                    gscr[t, hb][:, g * 256:(g + 1) * 256], [GSCR[t][hb]], [GSL[r3]])
            for nn in range(4):
                for k8 in range(8):
                    pe_mm(P[r3][:, nn * 128:(nn + 1) * 128], wu[wpar][:, k8, nn * 128:(nn + 1) * 128],
                          hnT[:, k8, t * 128:(t + 1) * 128], k8 == 0, k8 == 7, [WU[wpar], HNT[t]], [PB[r3]])
            if t == n - 1 and g + 1 < NG:
                load_group_cast(g + 1)

        def stage_mid(k):
            r3 = k % 3
            act(ge[r3][:], P[r3][:].rearrange("p (n t) -> p n t", n=4), AF.Gelu, [PB[r3]], [GE[r3]])
            tt("dve", ab[r3][:].rearrange("p n (h t) -> p n h t", h=2),
               ge[r3][:].rearrange("p n (h t) -> p n h t", h=2),
               gsl[r3][:].rearrange("p h n t -> p n h t"), ALU.mult, [GE[r3], GSL[r3]], [ABB[r3]])

        def stage_down(k):
            g, t = its[k]
            r3, par, wpar = k % 3, k % 2, g % 2
            for nn in range(4):
                for half in range(2):
                    b = 3 + par * 2 + half
                    pe_mm(P[b][:], ab[r3][:, nn, :], wd[wpar][:, nn, half * 512:(half + 1) * 512], nn == 0, nn == 3,
                          [ABB[r3], WD[wpar]], [PB[b]])
            for half in range(2):
                b = 3 + par * 2 + half
                tt("dve", h_sb[:, t, half * 512:(half + 1) * 512], h_sb[:, t, half * 512:(half + 1) * 512], P[b][:],
                   ALU.add, [HB[t], PB[b]], [HB[t]])

        NI = len(its)
        for k in range(NI + 2):
            if k < NI:
                stage_up(k)
            if 1 <= k < NI + 1:
                stage_mid(k - 1)
            if k >= 2:
                stage_down(k - 2)
        fw.barrier()
        p2.close()

        pf = contextlib.ExitStack()
        ob = [sb(pf, "f_o%d" % i, [128, 1024], F32) for i in range(2)]
        OB = [Buf(), Buf()]
        if final:
            gain = sb(pf, "f_gain", [128, 1024], F32)
            GB = Buf()
            junk = sb(pf, "f_junk", [128, 1024], BF16)
            st = sb(pf, "f_st", [128, 4], F32)
            JB, SB_ = Buf(), Buf()
            dma("sp", "gain", gain[:], g_fin, [], [GB])
            for t in range(n):
                rmsnorm((junk, JB, st, SB_), h_sb[:, t, :], [HB[t]], gain, GB, ob[t % 2][:], OB[t % 2])
                dma("sp", "out%d" % (t % 2), hout[(o0 + t) * 128:(o0 + t + 1) * 128, :], ob[t % 2][:], [OB[t % 2]], [])
        else:
            for t in range(n):
                at = o0 + XOFF + t
                ts("dve", ob[t % 2][:], h_sb[:, t, :], vmask[:, at:at + 1], None, ALU.mult, None, [HB[t], VM], [OB[t % 2]])
                dma("sp", "out%d" % (t % 2), dst[at * 128:(at + 1) * 128, :], ob[t % 2][:], [OB[t % 2]], [])
        fw.barrier()
        pf.close()
        pz.close()

    vmask = sb(top, 'vmask', [128, XT], F32)
    VM = Buf()
    dma('sp', 'vm', vmask[:], valid, [], [VM])
    zs = contextlib.ExitStack()
    zt = sb(zs, 'zt', [128, 1024], F32)
    ZB = Buf()
    fw.op('dve', lambda e: e.memset(zt[:], 0.0), [], [ZB])
    for d_ in range(2):
        for t in range(XT):
            dma('sp', 'z%d' % (t % 2), HD[d_][t * 128:(t + 1) * 128, :], zt[:], [ZB], [])
    fw.barrier()
    zs.close()
    src = hx
    for li in range(4):
        kind = 'attn' if li % 2 == 0 else 'conv'
        dst = HD[li % 2]
        for (o0, n) in CHUNKS[li]:
            emit_chunk(li, kind, src, dst, o0, n, li == 3)
        src = dst
    fw.finalize()
    top.close()
    return nc


_PROG = []


def _ext(h, c):
    b, q = c // 4, c % 4
    hb = h[b].reshape(128, 64, 1024)
    out = np.zeros((XT * 2, 64, 1024), np.float32)
    r0 = 32 * q - 2 * XOFF
    lo, hi = max(r0, 0), min(r0 + XT * 2, 128)
    out[lo - r0:hi - r0] = hb[lo:hi]
    return out.reshape(XT * 128, 1024)


def _btab(rel_bias, q):
    qc = np.arange(64)[:, None]
    kc = np.arange(64)[None, :]
    ws = np.clip(qc - 8, 0, 48)
    vcol = (kc >= ws) & (kc < ws + 16)
    coff = np.clip(kc - qc + 15, 0, 30)
    out = np.full((TROWS, 16, 64, 16, 64), NEG, np.float32)
    for idx in range(TROWS):
        rr = idx - 8
        r = 32 * q + rr
        if r < 0 or r >= 128:
            continue
        dk, nk = _rowwin(rr)
        kabs = r + dk
        rs = min(max(r - 4, 0), 120)
        for wr in range(nk):
            krow = kabs + wr
            if rs <= krow < rs + 8:
                roff = krow - r + 7
                vals = rel_bias[:, roff, :][:, coff]
                out[idx, :, :, wr, :] = np.where(vcol[None], vals, np.float32(NEG))
    out = out.reshape(TROWS, 8, 2, 64, 1024).transpose(0, 1, 3, 2, 4).reshape(TROWS, 8, 64, 2048)
    return np.ascontiguousarray(out)


def _rep(v):
    return np.ascontiguousarray(np.broadcast_to(np.asarray(v, np.float32)[None, :], (128, 1024)))


def kernel(**inp):
    if not _PROG:
        _PROG.append(build_fused())
    nc = _PROG[0]
    h = np.ascontiguousarray(np.asarray(inp["x"], np.float32))
    common = {"g_fin": _rep(inp["norm_final"])}
    tabs = {}
    for i in range(4):
        j = i // 2
        common["g_mix%d" % i] = _rep(inp["norm_mix"][i])
        common["g_ffn%d" % i] = _rep(inp["norm_ffn"][i])
        common["wq%d" % i] = np.ascontiguousarray(inp["peer_w_query"][i], np.float32)
        common["keysT%d" % i] = np.ascontiguousarray(
            np.asarray(inp["peer_sub_keys"][i], np.float32).reshape(16, 128, 128).transpose(2, 0, 1).reshape(128, 2048))
        common["wupT%d" % i] = np.ascontiguousarray(np.asarray(inp["peer_w_up"][i], np.float32).T)
        common["wdown%d" % i] = np.ascontiguousarray(inp["peer_w_down"][i], np.float32)
        if i % 2 == 1:
            common["w_in%d" % i] = np.ascontiguousarray(inp["conv_w_in"][j], np.float32)
            common["w_cv%d" % i] = np.ascontiguousarray(
                np.asarray(inp["conv_w_conv"][j], np.float32).reshape(3, 8, 128).transpose(2, 0, 1).reshape(128, 24))
            common["w_out%d" % i] = np.ascontiguousarray(inp["conv_w_out"][j], np.float32)
        else:
            common["w_qkv%d" % i] = np.ascontiguousarray(inp["attn_w_qkv"][j], np.float32)
            common["w_o%d" % i] = np.ascontiguousarray(inp["attn_w_o"][j], np.float32)
            rb = np.asarray(inp["attn_rel_bias"][j], np.float32)
            tabs[i] = [_btab(rb, q) for q in range(4)]
    in_maps = []
    for c in range(8):
        q = c % 4
        m = dict(common)
        m["hx"] = _ext(h, c)
        v = np.zeros((XT,), np.float32)
        for t in range(XT):
            at = 16 * q + t - XOFF
            v[t] = 1.0 if 0 <= at < 64 else 0.0
        m["valid"] = np.ascontiguousarray(np.broadcast_to(v[None, :], (128, XT)))
        for i in (0, 2):
            m["btab%d" % i] = tabs[i][q]
        in_maps.append(m)
    res = run_bass_kernel_spmd(nc, in_maps, core_ids=list(range(8)))
    out = np.empty((2, 8192, 1024), np.float32)
    for c in range(8):
        b, q = c // 4, c % 4
        out[b, q * 2048:(q + 1) * 2048] = np.asarray(res.results[c]["hout"], np.float32)
    return out
```
